# Optimizing a Trainium2 kernel written in Bass

```python
import math
import jax
import jax.numpy as jnp
from jax import lax
import numpy as np

D_MODEL = 1024
BATCH = 8
SEQ = 4096
DEPTH = 1
DEC_BATCH = 2
DEC_SEQ = 8192
PAST_LEN = 128

D_MIX = D_MODEL
ATT_WIDTH = D_MIX // 2
FNET_WIDTH = D_MIX - ATT_WIDTH
N_ATT_HEADS = 4
DV = ATT_WIDTH // N_ATT_HEADS
DK = DV // 2
N_FGROUPS = 4
FG_DIM = FNET_WIDTH // N_FGROUPS
PLE_DIM = 256
NUM_BUCKETS = 32
MAX_DISTANCE = 128
Q_BLOCK = 128
NORM_EPS = 1e-6
SUBLN_EPS = 1e-5
Q_COLS = N_ATT_HEADS * 2 * DK
K_COLS = N_ATT_HEADS * 2 * DK
V_COLS = ATT_WIDTH
GA_COLS = ATT_WIDTH
U_COLS = FNET_WIDTH
GF_COLS = FNET_WIDTH
IN_COLS = Q_COLS + K_COLS + V_COLS + GA_COLS + U_COLS + GF_COLS

kernel_name = "hymba_diffattn_fnet_encoder"


def rmsnorm(x, g, eps=NORM_EPS):
    xf = x.astype(jnp.float32)
    y = xf * lax.rsqrt(jnp.mean(xf * xf, axis=-1, keepdims=True) + eps)
    return (y * g.astype(jnp.float32)).astype(x.dtype)


def rel_buckets(q_pos, k_pos):
    rel = k_pos[None, :] - q_pos[:, None]
    half = NUM_BUCKETS // 2
    ret = jnp.where(rel > 0, half, 0)
    n = jnp.abs(rel)
    max_exact = half // 2
    nf = jnp.maximum(n, 1).astype(jnp.float32)
    large = max_exact + (jnp.log(nf / max_exact) / math.log(MAX_DISTANCE / max_exact)
                         * (half - max_exact)).astype(jnp.int32)
    large = jnp.minimum(large, half - 1)
    return ret + jnp.where(n < max_exact, n, large)


def diff_attention(q, k, v, lam, rel_bias):
    b, s = q.shape[0], q.shape[1]
    nblk = s // Q_BLOCK
    qb = q.reshape(b, nblk, Q_BLOCK, N_ATT_HEADS, 2, DK).transpose(1, 0, 2, 3, 4, 5)
    starts = jnp.arange(nblk, dtype=jnp.int32) * Q_BLOCK
    k_pos = jnp.arange(s, dtype=jnp.int32)
    table = rel_bias.astype(jnp.float32)
    scale = 1.0 / math.sqrt(DK)

    def block(args):
        qi, s0 = args
        q_pos = s0 + jnp.arange(Q_BLOCK, dtype=jnp.int32)
        bias = table[rel_buckets(q_pos, k_pos)].transpose(2, 0, 1)
        logits = (jnp.einsum('bqhcd,bkhcd->bhcqk', qi, k).astype(jnp.float32) * scale
                  + bias[None, :, None])
        probs = jax.nn.softmax(logits, axis=-1)
        w = probs[:, :, 0] - lam * probs[:, :, 1]
        return jnp.einsum('bhqk,bkhd->bqhd', w.astype(v.dtype), v)

    o = lax.map(block, (qb, starts))
    return o.transpose(1, 0, 2, 3, 4).reshape(b, s, N_ATT_HEADS, DV)


def fourier_mix(u, w_f):
    b, s = u.shape[0], u.shape[1]
    ug = u.reshape(b, s, N_FGROUPS, FG_DIM).astype(jnp.float32)
    f = jnp.real(jnp.fft.fft2(ug, axes=(1, 3), norm='ortho')).astype(u.dtype)
    out = jnp.einsum('bsgc,gcd->bsgd', f, w_f)
    return out.reshape(b, s, FNET_WIDTH)


def layer(h, p_i, i, norm_mix_g, w_in, lambda_params, subln_g, rel_bias,
          w_fourier, w_out, ple_norm_g, w_ple_gate, w_ple_proj):
    b, s = h.shape[0], h.shape[1]
    xn = rmsnorm(h, norm_mix_g)
    z = xn @ w_in
    o0 = 0
    q = z[..., o0:o0 + Q_COLS]; o0 += Q_COLS
    k = z[..., o0:o0 + K_COLS]; o0 += K_COLS
    v = z[..., o0:o0 + V_COLS]; o0 += V_COLS
    g_att = z[..., o0:o0 + GA_COLS]; o0 += GA_COLS
    u = z[..., o0:o0 + U_COLS]; o0 += U_COLS
    g_f = z[..., o0:o0 + GF_COLS]

    lambda_init = 0.8 - 0.6 * math.exp(-0.3 * i)
    lp = lambda_params.astype(jnp.float32)
    lam = jnp.exp(jnp.sum(lp[0] * lp[1])) - jnp.exp(jnp.sum(lp[2] * lp[3])) + lambda_init
    q = q.reshape(b, s, N_ATT_HEADS, 2, DK)
    k = k.reshape(b, s, N_ATT_HEADS, 2, DK)
    v = v.reshape(b, s, N_ATT_HEADS, DV)
    att = diff_attention(q, k, v, lam, rel_bias)
    att = rmsnorm(att, subln_g, SUBLN_EPS) * (1.0 - lambda_init)
    att = att.reshape(b, s, ATT_WIDTH) * jax.nn.silu(g_att)

    fno = fourier_mix(u, w_fourier) * jax.nn.silu(g_f)

    h = h + jnp.concatenate([att, fno], axis=-1) @ w_out

    gate = jax.nn.sigmoid(rmsnorm(h, ple_norm_g) @ w_ple_gate)
    h = h + gate * (p_i @ w_ple_proj)
    return h


def trunk(x, p, norm_mix_g, w_in, lambda_params, subln_g, rel_bias, w_fourier,
          w_out, ple_norm_g, w_ple_gate, w_ple_proj, final_norm_g):
    h = x
    for i in range(DEPTH):
        h = layer(h, p[i], i, norm_mix_g[i], w_in[i], lambda_params[i], subln_g[i],
                  rel_bias, w_fourier[i], w_out[i], ple_norm_g[i], w_ple_gate[i],
                  w_ple_proj[i])
    return rmsnorm(h, final_norm_g)


def setup_inputs(seed: int = 0) -> dict:
    key = jax.random.key(seed)
    ks = jax.random.split(key, 16)
    f32 = jnp.float32
    nrm = lambda k, shape, sc: jax.random.normal(k, shape, f32) * sc
    return {
        "x_prompt": nrm(ks[0], (BATCH, SEQ, D_MODEL), 1.0),
        "x_sample": nrm(ks[1], (DEC_BATCH, DEC_SEQ, D_MODEL), 1.0),
        "p_prompt": nrm(ks[2], (DEPTH, BATCH, SEQ, PLE_DIM), 1.0),
        "p_sample": nrm(ks[3], (DEPTH, DEC_BATCH, DEC_SEQ, PLE_DIM), 1.0),
        "norm_mix_g": 1.0 + nrm(ks[4], (DEPTH, D_MODEL), 0.05),
        "w_in": nrm(ks[5], (DEPTH, D_MODEL, IN_COLS), D_MODEL ** -0.5),
        "lambda_params": nrm(ks[6], (DEPTH, 4, DK), 0.1),
        "subln_g": 1.0 + nrm(ks[7], (DEPTH, DV), 0.05),
        "rel_bias": nrm(ks[8], (NUM_BUCKETS, N_ATT_HEADS), 0.5),
        "w_fourier": nrm(ks[9], (DEPTH, N_FGROUPS, FG_DIM, FG_DIM), FG_DIM ** -0.5),
        "w_out": nrm(ks[10], (DEPTH, D_MIX, D_MODEL), D_MIX ** -0.5),
        "ple_norm_g": 1.0 + nrm(ks[11], (DEPTH, D_MODEL), 0.05),
        "w_ple_gate": nrm(ks[12], (DEPTH, D_MODEL, D_MODEL), D_MODEL ** -0.5),
        "w_ple_proj": nrm(ks[13], (DEPTH, PLE_DIM, D_MODEL), PLE_DIM ** -0.5),
        "final_norm_g": 1.0 + nrm(ks[14], (D_MODEL,), 0.05),
    }


def reference(x_prompt, x_sample, p_prompt, p_sample, norm_mix_g, w_in, lambda_params,
              subln_g, rel_bias, w_fourier, w_out, ple_norm_g, w_ple_gate, w_ple_proj,
              final_norm_g):
    y_prompt = trunk(x_prompt, p_prompt, norm_mix_g, w_in, lambda_params, subln_g,
                     rel_bias, w_fourier, w_out, ple_norm_g, w_ple_gate, w_ple_proj,
                     final_norm_g)
    y_sample = trunk(x_sample, p_sample, norm_mix_g, w_in, lambda_params, subln_g,
                     rel_bias, w_fourier, w_out, ple_norm_g, w_ple_gate, w_ple_proj,
                     final_norm_g)
    return (y_prompt, y_sample)
```

```python
import contextlib
import math
import numpy as np
import ml_dtypes
import concourse.bass as bass
import concourse.mybir as mybir
from concourse.bass_utils import run_bass_kernel_spmd

F32 = mybir.dt.float32
BF16 = mybir.dt.bfloat16
AF = mybir.ActivationFunctionType
ALU = mybir.AluOpType
AX = mybir.AxisListType

D = 1024
NCORES = 8
JOBS = [dict(NKV=4096, NQ=4096, N2=32, R=4, NK2=32), dict(NKV=8192, NQ=2048, N2=64, R=2, NK2=16)]
NCOL = 67


class Buf:
    __slots__ = ("w", "r", "const")

    def __init__(self, const=False):
        self.w = {}
        self.r = {}
        self.const = const


def _merge(d, s):
    for k, v in s.items():
        if d.get(k, -1) < v:
            d[k] = v


class Rec:
    ENGS = ("pe", "act", "dve", "pool", "sp")

    def __init__(self, nc, stack):
        self.nc = nc
        self.stack = stack
        self.sems = {e: stack.enter_context(nc.semaphore("s_" + e)) for e in self.ENGS if e != "sp"}
        self.base_cnt = {e: 0 for e in self.ENGS}
        self.dcnt = {}
        self.reset()

    def reset(self):
        self.ops = {e: [] for e in self.ENGS}

    def op(self, eng, fn, reads=(), writes=(), dma_key=None):
        deps = {}
        for b in reads:
            _merge(deps, b.w)
        for b in writes:
            _merge(deps, b.r)
            _merge(deps, b.w)
        lst = self.ops[eng]
        idx = len(lst)
        if dma_key is None:
            tk = {eng: idx}
        else:
            n = self.dcnt.get(dma_key, 0) + 16
            self.dcnt[dma_key] = n
            tk = {"d:" + dma_key: n}
            if ("d:" + dma_key) not in self.sems:
                self.sems["d:" + dma_key] = self.stack.enter_context(self.nc.semaphore("d_" + dma_key))
        lst.append((fn, deps, dma_key))
        for b in reads:
            if not b.const:
                _merge(b.r, tk)
        for b in writes:
            b.w = dict(tk)
            b.r = {}
        return tk

    def emit(self, bufs_to_clear=()):
        nc = self.nc
        needed = {e: set() for e in self.ENGS}
        for e in self.ENGS:
            for (fn, deps, dk) in self.ops[e]:
                for ch, v in deps.items():
                    if ch in needed:
                        if ch == "pe" and e == "pe":
                            continue
                        needed[ch].add(v)
            if e != "sp":
                for i in range(len(self.ops[e]) - 1, -1, -1):
                    if self.ops[e][i][2] is None:
                        needed[e].add(i)
                        break
        cum = {}
        for e in self.ENGS:
            c = self.base_cnt[e]
            arr = []
            nd = needed[e]
            for i in range(len(self.ops[e])):
                if i in nd and self.ops[e][i][2] is None:
                    c += 1
                arr.append(c)
            cum[e] = arr
        final = {e: (cum[e][-1] if cum[e] else self.base_cnt[e]) for e in self.ENGS if e != "sp"}
        dfinal = {"d:" + k: v for k, v in self.dcnt.items()}
        sems = self.sems
        with nc.Block() as block:
            handles = {"pe": block.tensor, "act": block.scalar, "dve": block.vector,
                       "pool": block.gpsimd, "sp": block.sync}

            def run(e):
                def body(h):
                    seen = {}
                    for i, (fn, deps, dk) in enumerate(self.ops[e]):
                        for ch, v in deps.items():
                            if ch in cum:
                                if ch == "pe" and e == "pe":
                                    continue
                                val = cum[ch][v]
                            else:
                                val = v
                            if seen.get(ch, 0) < val:
                                h.wait_ge(sems[ch], val)
                                seen[ch] = val
                        inst = fn(h)
                        if dk is not None:
                            inst.then_inc(sems["d:" + dk], 16)
                        elif i in needed[e]:
                            inst.then_inc(sems[e], 1)
                    for ch, val in list(final.items()) + list(dfinal.items()):
                        if ch == e:
                            continue
                        if val > 0 and seen.get(ch, 0) < val:
                            h.wait_ge(sems[ch], val)
                return body

            for e in self.ENGS:
                handles[e](run(e))
        for e in final:
            self.base_cnt[e] = final[e]
        self.reset()
        for b in bufs_to_clear:
            b.w = {}
            b.r = {}


class BufSet:
    def __init__(self):
        self.d = {}

    def __getitem__(self, k):
        b = self.d.get(k)
        if b is None:
            b = Buf()
            self.d[k] = b
        return b

    def clear(self):
        self.d = {}


def build_program(dbg=None, njobs=2):
    nc = bass.Bass("TRN2", target_bir_lowering=False)
    dt_in = lambda n, s, d=F32: nc.dram_tensor(n, list(s), d, kind="ExternalInput")
    xj = [dt_in("xp", [4096, D]), dt_in("xs", [8192, D])]
    pj = [dt_in("pp", [4096, 256]), dt_in("psm", [2048, 256])]
    yj = [nc.dram_tensor("yp", [4096, D], F32, kind="ExternalOutput"),
          nc.dram_tensor("ys", [2048, D], F32, kind="ExternalOutput")]
    win = dt_in("win", [D, 3072]); wout = dt_in("wout", [D, D]); wgate = dt_in("wgate", [D, D])
    wproj = dt_in("wproj", [256, D]); wf = dt_in("wf", [4, 128, 128])
    gmix_d = dt_in("gmix", [1, D]); gple_d = dt_in("gple", [1, D]); gfin_d = dt_in("gfin", [1, D])
    gsub_d = dt_in("gsub", [1, 128]); lam_d = dt_in("lamp", [1, 256]); relb_d = dt_in("relb", [32, 4])
    ident_d = dt_in("ident", [128, 128], BF16); J_d = dt_in("Jm", [128, 128]); CS_d = dt_in("CS", [128, 256])
    F1_d = dt_in("F1ab", [128, 512], BF16)
    tw_d = [dt_in("tw0", [128, 3, 128]), dt_in("tw1", [128, 3, 128])]
    F2_d = [dt_in("F20", [128, 2, 32], BF16), dt_in("F21", [128, 2, 32], BF16)]
    OHG_d = dt_in("OHG", [32, 3, 1280]); OHC_d = dt_in("OHC", [32, NCOL])
    kw = dict(kind="ExternalOutput") if dbg else {}
    Gd = nc.dram_tensor("Gd", [3, 4, 1280], F32, **kw)
    KTd = nc.dram_tensor("KTd", [4, 128, 8192], BF16, **kw)
    QTd = nc.dram_tensor("QTd", [4, 128, 4096], BF16, **kw)
    UTd = nc.dram_tensor("UTd", [4, 128, 8192], BF16, **kw)
    Vd = nc.dram_tensor("Vd", [4, 128, 64, 129], BF16, **kw)
    G2d = nc.dram_tensor("G2d", [4, 128, 32, 128], BF16, **kw)
    GFd = nc.dram_tensor("GFd", [4, 128, 32, 128], BF16, **kw)
    if dbg:
        dcb = nc.dram_tensor("dcb", [128, 4 * NCOL], F32, kind="ExternalOutput")
        dbt = nc.dram_tensor("dbt", [128, 4 * 1024], BF16, kind="ExternalOutput")
        dcsw = nc.dram_tensor("dcsw", [128, 4 * 256], BF16, kind="ExternalOutput")
        dlam = nc.dram_tensor("dlam", [128, 4], F32, kind="ExternalOutput")
        dmix = nc.dram_tensor("dmix", [128, 8 * 4096], BF16, kind="ExternalOutput")

    with contextlib.ExitStack() as G:
        _cnt = [0]

        def sbt(st, n, s, d):
            _cnt[0] += 1
            return st.enter_context(nc.sbuf_tensor("sb%d_%s" % (_cnt[0], n), list(s), d))
        R = Rec(nc, G)
        B = BufSet()
        ps = G.enter_context(nc.psum_tensor("ps", [128, 8, 512], F32))

        def psb(bank):
            return ps[:, bank, :].bitcast(BF16)

        ident = sbt(G, "ident", [128, 128], BF16)
        Jf = sbt(G, "Jf", [128, 128], F32)
        F1ab = sbt(G, "F1ab", [128, 512], BF16)
        tw = [sbt(G, "tw0", [128, 3, 128], F32), sbt(G, "tw1", [128, 3, 128], F32)]
        twb = [sbt(G, "twb0", [128, 2, 2, 128], BF16), sbt(G, "twb1", [128, 2, 2, 128], BF16)]
        F2 = [sbt(G, "F20", [128, 2, 32], BF16), sbt(G, "F21", [128, 2, 32], BF16)]
        cb = sbt(G, "cb", [128, 4, NCOL], F32)
        lamc = sbt(G, "lamc", [128, 4], F32)
        gsub2 = sbt(G, "gsub2", [128, 128], F32)
        BTw = sbt(G, "BTw", [128, 4, 1024], BF16)
        cb8 = sbt(G, "cb8", [128, 4, NCOL], F32)
        CSW = sbt(G, "CSW", [128, 4, 256], BF16)
        gvec = sbt(G, "gvec", [128, 3, D], F32)
        epsc = sbt(G, "epsc", [128, 2], F32)

        def dma(out, in_, key, reads=(), writes=(), eng="sp"):
            return R.op(eng, lambda h: h.dma_start(out=out, in_=in_), reads=reads, writes=writes, dma_key=key)

        with contextlib.ExitStack() as S:
            relb = sbt(S, "relb", [32, 4], F32)
            OHC = sbt(S, "OHC", [32, NCOL], F32)
            OHG = sbt(S, "OHG", [32, 3, 1280], F32)
            rhs4 = sbt(S, "rhs4", [32, 4, NCOL], F32)
            ones32 = sbt(S, "ones32", [32, 128], F32)
            Gsb = sbt(S, "Gsb", [4, 3, 1280], F32)
            Hsb4 = sbt(S, "Hsb", [128, 4, 2176], F32)
            CS = sbt(S, "CS", [128, 256], F32)
            wfs = sbt(S, "wfs", [128, 4, 128], F32)
            lpb = sbt(S, "lpb", [128, 256], F32)
            lpt = sbt(S, "lpt", [128, 2, 64], F32)
            lps = sbt(S, "lps", [128, 2], F32)
            gsb_ = sbt(S, "gsb_", [128, 128], F32)

            dma(ident[:, :], ident_d.ap()[:, :], "c0", writes=[B["ident"]])
            dma(Jf[:, :], J_d.ap()[:, :], "c1", writes=[B["Jf"]])
            dma(F1ab[:, :], F1_d.ap()[:, :], "c2", writes=[B["F1"]])
            for j in range(2):
                dma(tw[j][:, :, :], tw_d[j].ap()[:, :, :], "c3_%d" % j, writes=[B["tw%d" % j]])
                dma(F2[j][:, :, :], F2_d[j].ap()[:, :, :], "c4_%d" % j, writes=[B["F2%d" % j]])
            dma(relb[:, :], relb_d.ap()[:, :], "c5", writes=[B["relb"]])
            dma(OHC[:, :], OHC_d.ap()[:, :], "c6", writes=[B["OHC"]])
            dma(OHG[:, :, :], OHG_d.ap()[:, :, :], "c7", writes=[B["OHG"]])
            dma(CS[:, :], CS_d.ap()[:, :], "c8", writes=[B["CS"]])
            dma(wfs[:, :, :], wf.ap().rearrange("g c d -> c g d"), "c9", writes=[B["wfs"]])
            dma(lpb[:, :], bass.AP(lam_d, 0, [[0, 128], [1, 256]]), "c10", writes=[B["lpb"]])
            dma(gsb_[:, :], bass.AP(gsub_d, 0, [[0, 128], [1, 128]]), "c11", writes=[B["gsb"]])
            for i, gd in enumerate([gmix_d, gple_d, gfin_d]):
                dma(gvec[:, i, :], bass.AP(gd, 0, [[0, 128], [1, D]]), "c12_%d" % i, writes=[B["gvec%d" % i]])
            for j in range(2):
                R.op("dve", lambda h, j=j: h.tensor_copy(out=twb[j][:, 0, :, :], in_=tw[j][:, 0:2, :]), reads=[B["tw%d" % j]], writes=[B["twb%d" % j]])
                R.op("dve", lambda h, j=j: h.tensor_copy(out=twb[j][:, 1, :, :], in_=tw[j][:, 1:3, :]), reads=[B["tw%d" % j]], writes=[B["twb%d" % j]])
            R.op("pool", lambda h: h.memset(epsc[:, 0:1], 1e-6), writes=[B["eps"]])
            R.op("pool", lambda h: h.memset(epsc[:, 1:2], 1e-5), writes=[B["eps"]])
            R.op("pool", lambda h: h.memset(ones32[:, :], 1.0), writes=[B["ones32"]])
            R.op("pool", lambda h: h.tensor_scalar(out=gsub2[:, :], in0=gsb_[:, :], scalar1=0.8, scalar2=None, op0=ALU.mult),
                 reads=[B["gsb"]], writes=[B["gsub2"]])
            lpv = lpb[:, :].rearrange("p (a b c) -> p a b c", a=2, b=2)
            R.op("dve", lambda h: h.tensor_tensor(out=lpt[:, :, :], in0=lpv[:, :, 0, :], in1=lpv[:, :, 1, :], op=ALU.mult),
                 reads=[B["lpb"]], writes=[B["lpt"]])
            R.op("dve", lambda h: h.tensor_reduce(out=lps[:, :], in_=lpt[:, :, :], axis=AX.X, op=ALU.add),
                 reads=[B["lpt"]], writes=[B["lps"]])
            R.op("act", lambda h: h.activation(out=lamc[:, 0:2], in_=lps[:, :], func=AF.Exp), reads=[B["lps"]], writes=[B["lamc"]])
            R.op("dve", lambda h: h.tensor_tensor(out=lamc[:, 2:3], in0=lamc[:, 0:1], in1=lamc[:, 1:2], op=ALU.subtract),
                 reads=[B["lamc"]], writes=[B["lamc"]])
            R.op("dve", lambda h: h.tensor_scalar(out=lamc[:, 3:4], in0=lamc[:, 2:3], scalar1=0.2, scalar2=-1.0, op0=ALU.add, op1=ALU.mult),
                 reads=[B["lamc"]], writes=[B["lamc"]])
            for hh in range(4):
                R.op("dve", lambda h, hh=hh: h.tensor_scalar(out=rhs4[:, hh, :], in0=OHC[:, :], scalar1=relb[:, hh:hh + 1], scalar2=None, op0=ALU.mult),
                     reads=[B["OHC"], B["relb"]], writes=[B["rhs4"]])
            R.op("pe", lambda h: h.matmul(ps[:, 0, 0:4 * NCOL], lhsT=ones32[:, :], rhs=rhs4[:, :, :].rearrange("p a b -> p (a b)"), start=True, stop=True),
                 reads=[B["ones32"], B["rhs4"]], writes=[B["ps0"]])
            R.op("dve", lambda h: h.tensor_copy(out=cb[:, :, :].rearrange("p a b -> p (a b)"), in_=ps[:, 0, 0:4 * NCOL]), reads=[B["ps0"]], writes=[B["cb"]])
            R.op("dve", lambda h: h.tensor_scalar(out=cb8[:, :, :].rearrange("p a b -> p (a b)"), in0=cb[:, :, :].rearrange("p a b -> p (a b)"),
                                                  scalar1=8.0, scalar2=None, op0=ALU.mult), reads=[B["cb"]], writes=[B["cb8"]])
            bank_of = {}
            bi = 1
            for v in range(3):
                for (c0, c1) in [(0, 512), (512, 1024), (1024, 1280)]:
                    bnk = 1 + (bi - 1) % 7
                    bi += 1
                    R.op("pe", lambda h, v=v, c0=c0, c1=c1, bnk=bnk: h.matmul(ps[0:4, bnk, 0:c1 - c0], lhsT=relb[:, :], rhs=OHG[:, v, c0:c1], start=True, stop=True),
                         reads=[B["relb"], B["OHG"]], writes=[B["psb%d" % bnk]])
                    R.op("dve", lambda h, v=v, c0=c0, c1=c1, bnk=bnk: h.tensor_copy(out=Gsb[:, v, c0:c1], in_=ps[0:4, bnk, 0:c1 - c0]),
                         reads=[B["psb%d" % bnk]], writes=[B["Gsb"]])
            dma(Gd.ap().rearrange("v h t -> h v t"), Gsb[:, :, :], "c13", reads=[B["Gsb"]], writes=[B["Gd"]])
            for hh in range(4):
                dma(Hsb4[:, hh, 0:1152], bass.AP(Gd, (0 * 4 + hh) * 1280, [[1, 128], [1, 1152]]), "c14_%d" % hh, reads=[B["Gd"]], writes=[B["Hsb%d" % hh]])
                dma(Hsb4[:, hh, 1152:1664], bass.AP(Gd, (1 * 4 + hh) * 1280, [[1, 128], [1, 512]]), "c14_%d" % hh, reads=[B["Gd"]], writes=[B["Hsb%d" % hh]])
                dma(Hsb4[:, hh, 1664:2176], bass.AP(Gd, (2 * 4 + hh) * 1280 + 640, [[1, 128], [1, 512]]), "c14_%d" % hh, reads=[B["Gd"]], writes=[B["Hsb%d" % hh]])
            for hh in range(4):
                bA, bB = 1 + 2 * (hh % 2), 2 + 2 * (hh % 2)
                R.op("pe", lambda h, hh=hh, bA=bA: h.matmul(ps[:, bA, 0:384], lhsT=Jf[:, :], rhs=Hsb4[:, hh, 384:768], start=True, stop=True),
                     reads=[B["Jf"], B["Hsb%d" % hh]], writes=[B["psb%d" % bA]])
                R.op("pe", lambda h, hh=hh, bA=bA: h.matmul(ps[:, bA, 384:512], lhsT=Jf[:, :], rhs=Hsb4[:, hh, 1152 + 384:1152 + 512], start=True, stop=True),
                     reads=[B["Jf"], B["Hsb%d" % hh]], writes=[B["psb%d" % bA]])
                R.op("pe", lambda h, hh=hh, bB=bB: h.matmul(ps[:, bB, 0:128], lhsT=Jf[:, :], rhs=Hsb4[:, hh, 1664:1792], start=True, stop=True),
                     reads=[B["Jf"], B["Hsb%d" % hh]], writes=[B["psb%d" % bB]])
                for (dst0, n, bank, src0, col) in [(0, 384, bA, 0, 64), (384, 384, bA, 0, 65), (768, 128, bA, 384, 16), (896, 128, bB, 0, 63)]:
                    R.op("dve", lambda h, hh=hh, dst0=dst0, n=n, bank=bank, src0=src0, col=col: h.tensor_scalar(
                        out=BTw[:, hh, dst0:dst0 + n], in0=ps[:, bank, src0:src0 + n], scalar1=cb8[:, hh, col:col + 1], scalar2=None, op0=ALU.subtract),
                         reads=[B["psb%d" % bank], B["cb8"]], writes=[B["BTw"]])
            for g in range(4):
                for q in range(2):
                    R.op("pe", lambda h, g=g, q=q: h.matmul(ps[:, 6, q * 128:(q + 1) * 128], lhsT=CS[:, q * 128:(q + 1) * 128], rhs=wfs[:, g, :], start=True, stop=True),
                         reads=[B["CS"], B["wfs"]], writes=[B["psb6"]])
                R.op("dve", lambda h, g=g: h.tensor_copy(out=CSW[:, g, :], in_=ps[:, 6, 0:256]), reads=[B["psb6"]], writes=[B["CSW"]])
            R.emit()
            B.clear()
            if dbg:
                dma(dcb.ap()[:, :], cb[:, :, :].rearrange("p a b -> p (a b)"), "g0")
                dma(dbt.ap()[:, :], BTw[:, :, :].rearrange("p a b -> p (a b)"), "g1")
                dma(dcsw.ap()[:, :], CSW[:, :, :].rearrange("p a b -> p (a b)"), "g2")
                dma(dlam.ap()[:, :], lamc[:, :], "g3")
                R.emit()
                if dbg == "S":
                    return nc

        def load_weight(wst, dst, src_ap, nch, ncols, name, cw=1024):
            for c in range(nch):
                dma(dst[:, c, 0:ncols], src_ap[c * 128:(c + 1) * 128, 0:ncols], "%s_%d" % (name, c), eng="pool", writes=[B[name + "_%d" % c]])

        def rstd_ops(ss_ap, out_ap, scale, eps_ap, tag, bufs_r, buf_w, tmp_ap):
            R.op("act", lambda h: h.activation(out=tmp_ap, in_=ss_ap, func=AF.Ln, bias=eps_ap, scale=scale),
                 reads=bufs_r + [B["epsc"]], writes=[B[tag + "_t"]])
            R.op("act", lambda h: h.activation(out=out_ap, in_=tmp_ap, func=AF.Exp, scale=-0.5),
                 reads=[B[tag + "_t"]], writes=[buf_w])

        for ji, job in enumerate(JOBS[:njobs]):
            NKV, NQ, N2, RR, NK2 = job["NKV"], job["NQ"], job["N2"], job["R"], job["NK2"]
            NB = NKV // 512
            NBQ = NQ // 512
            x_ap = xj[ji].ap()
            with contextlib.ExitStack() as A:
                Wb = sbt(A, "Wb", [128, 8, 3072], BF16)
                load_weight(None, Wb, win.ap(), 8, 3072, "Wb")
                xt = sbt(A, "xt", [128, 4, D], F32)
                junk = sbt(A, "junk", [128, D], BF16)
                xnb = sbt(A, "xnb", [128, 4, D], BF16)
                xnT = sbt(A, "xnT", [128, 2, 8, 512], BF16)
                ssA = sbt(A, "ssA", [128, 3, 3, 4], F32)
                FMst = sbt(A, "FMst", [128, 2, 12, 512], BF16)
                TMst = sbt(A, "TMst", [128, 2, 4, 4, 129], BF16)
                G2st = sbt(A, "G2st", [128, 2, 4, 4, 128], BF16)
                GFst = sbt(A, "GFst", [128, 2, 4, 4, 128], BF16)
                g2f = sbt(A, "g2f", [128, 2, 512], F32)
                for s in range(2):
                    R.op("pool", lambda h, s=s: h.memset(TMst[:, s, :, :, 128:129], 1.0), writes=[B["TMst%d" % s]])

                def FE1_items(blk):
                    b3 = blk % 3
                    items = []
                    items.append(lambda: R.op("dve", lambda h, b3=b3: h.memset(ssA[:, b3, 0, :], 0.0), writes=[B["ss%d" % b3]]))
                    for ti in range(4):
                        items.append(lambda ti=ti: R.op("act", lambda h, b3=b3, ti=ti: h.activation(out=junk[:, :], in_=xt[:, ti, :], func=AF.Square,
                                                                                            accum_out=ssA[:, b3, 0, ti:ti + 1]),
                                                        reads=[B["xt%d" % ti]], writes=[B["junk"], B["ss%d" % b3]]))
                    items.append(lambda: rstd_ops(ssA[:, b3, 0, :], ssA[:, b3, 2, :], 1.0 / D, epsc[:, 0:1], "rA%d" % b3, [B["ss%d" % b3]], B["rs%d" % b3], ssA[:, b3, 1, :]))
                    for ti in range(4):
                        items.append(lambda ti=ti: R.op("dve", lambda h, ti=ti, b3=b3: h.scalar_tensor_tensor(out=xnb[:, ti, :], in0=xt[:, ti, :], scalar=ssA[:, b3, 2, ti:ti + 1],
                                                                                                   in1=gvec[:, 0, :], op0=ALU.mult, op1=ALU.mult),
                                                        reads=[B["xt%d" % ti], B["rs%d" % b3]], writes=[B["xnb%d" % ti]]))
                    return items

                def FE1(blk):
                    for it in FE1_items(blk):
                        it()

                def FE2_tile(blk, ti):
                    bs = blk % 2
                    tb = (0, 7)[ti % 2]
                    for c in range(8):
                        R.op("pe", lambda h, ti=ti, c=c, tb=tb: h.transpose(out=psb(tb)[:, c * 128:(c + 1) * 128], in_=xnb[:, ti, c * 128:(c + 1) * 128], identity=ident[:, :]),
                             reads=[B["xnb%d" % ti]], writes=[B["pb%d" % tb]])
                    if ti % 2 == 0:
                        R.op("act", lambda h, bs=bs, ti=ti, tb=tb: h.activation(out=xnT[:, bs, :, ti * 128:(ti + 1) * 128],
                                                                                in_=psb(tb).rearrange("p (c t) -> p c t", c=8), func=AF.Copy),
                             reads=[B["pb%d" % tb]], writes=[B["xnT%d" % bs]])
                    else:
                        R.op("dve", lambda h, bs=bs, ti=ti, tb=tb: h.tensor_copy(out=xnT[:, bs, :, ti * 128:(ti + 1) * 128],
                                                                                 in_=psb(tb).rearrange("p (c t) -> p c t", c=8)),
                             reads=[B["pb%d" % tb]], writes=[B["xnT%d" % bs]])

                def FE2(blk):
                    for ti in range(4):
                        FE2_tile(blk, ti)

                def loads(blk):
                    for ti in range(4):
                        t = blk * 4 + ti
                        dma(xt[:, ti, :], x_ap[t * 128:(t + 1) * 128, :], "x%d" % ti, writes=[B["xt%d" % ti]])

                fmk = [0]

                def TM_tile(blk, ti):
                    mine = blk < NBQ
                    bs = blk % 2
                    vb = 3 + 3 * (ti % 2)
                    for c in range(8):
                        R.op("pe", lambda h, bs=bs, ti=ti, c=c, vb=vb: h.matmul(ps[:, vb, :], lhsT=xnT[:, bs, c, ti * 128:(ti + 1) * 128], rhs=Wb[:, c, 1024:1536],
                                                                                 start=(c == 0), stop=(c == 7)),
                             reads=[B["xnT%d" % bs], B["Wb_%d" % c]], writes=[B["pb%d" % vb]])
                    R.op("dve", lambda h, bs=bs, ti=ti, vb=vb: h.tensor_copy(out=TMst[:, bs, :, ti, 0:128], in_=ps[:, vb, :].rearrange("p (a b) -> p a b", a=4)),
                         reads=[B["pb%d" % vb]], writes=[B["TMst%d" % bs]])
                    if mine:
                        gb = 4
                        gs = ti % 2
                        for c in range(8):
                            R.op("pe", lambda h, bs=bs, ti=ti, c=c, gb=gb: h.matmul(ps[:, gb, :], lhsT=xnT[:, bs, c, ti * 128:(ti + 1) * 128],
                                                                                  rhs=Wb[:, c, 1536:2048], start=(c == 0), stop=(c == 7)),
                                 reads=[B["xnT%d" % bs], B["Wb_%d" % c]], writes=[B["pb%d" % gb]])
                        R.op("act", lambda h, gb=gb, gs=gs: h.activation(out=g2f[:, gs, :], in_=ps[:, gb, :], func=AF.Silu),
                             reads=[B["pb%d" % gb]], writes=[B["g2f%d" % gs]])
                        R.op("dve", lambda h, bs=bs, ti=ti, gs=gs: h.tensor_tensor(out=G2st[:, bs, :, ti, :], in0=g2f[:, gs, :].rearrange("p (a b) -> p a b", a=4),
                                                                                  in1=gsub2[:, :].unsqueeze(1).to_broadcast([128, 4, 128]), op=ALU.mult),
                             reads=[B["g2f%d" % gs]], writes=[B["G2st%d" % bs]])
                        for c in range(8):
                            R.op("pe", lambda h, bs=bs, ti=ti, c=c: h.matmul(ps[:, 5, :], lhsT=xnT[:, bs, c, ti * 128:(ti + 1) * 128],
                                                                           rhs=Wb[:, c, 2560:3072], start=(c == 0), stop=(c == 7)),
                                 reads=[B["xnT%d" % bs], B["Wb_%d" % c]], writes=[B["pb5"]])
                        R.op("act", lambda h, bs=bs, ti=ti: h.activation(out=GFst[:, bs, :, ti, :], in_=ps[:, 5, :].rearrange("p (a b) -> p a b", a=4), func=AF.Silu),
                             reads=[B["pb5"]], writes=[B["GFst%d" % bs]])

                def FM(blk, items):
                    mine = blk < NBQ
                    bs = blk % 2
                    chunks = [(512 + hh * 128, hh) for hh in range(4)] + [(2048 + g * 128, 4 + g) for g in range(4)]
                    if mine:
                        chunks += [(hh * 128, 8 + hh) for hh in range(4)]
                    items = list(items)
                    for ci, (c0, idx) in enumerate(chunks):
                        fb = 1 + fmk[0] % 2
                        for c in range(8):
                            R.op("pe", lambda h, bs=bs, c=c, c0=c0, fb=fb: h.matmul(ps[:, fb, :], lhsT=Wb[:, c, c0:c0 + 128], rhs=xnT[:, bs, c, :], start=(c == 0), stop=(c == 7)),
                                 reads=[B["xnT%d" % bs], B["Wb_%d" % c]], writes=[B["pb%d" % fb]])
                        if fmk[0] % 2 == 0:
                            R.op("act", lambda h, bs=bs, idx=idx, fb=fb: h.activation(out=FMst[:, bs, idx, :], in_=ps[:, fb, :], func=AF.Copy),
                                 reads=[B["pb%d" % fb]], writes=[B["FMst%d" % bs]])
                        else:
                            R.op("dve", lambda h, bs=bs, idx=idx, fb=fb: h.tensor_copy(out=FMst[:, bs, idx, :], in_=ps[:, fb, :]),
                                 reads=[B["pb%d" % fb]], writes=[B["FMst%d" % bs]])
                        fmk[0] += 1
                        rem_chunks = len(chunks) - ci
                        ntake = -(-len(items) // rem_chunks)
                        for _ in range(ntake):
                            items.pop(0)()
                    tsl = slice(blk * 512, (blk + 1) * 512)
                    dma(KTd.ap()[:, :, tsl].rearrange("h p t -> p h t"), FMst[:, bs, 0:4, :], "oK%d" % bs, eng="pool", reads=[B["FMst%d" % bs]])
                    dma(UTd.ap()[:, :, tsl].rearrange("h p t -> p h t"), FMst[:, bs, 4:8, :], "oU%d" % bs, eng="pool", reads=[B["FMst%d" % bs]])
                    dma(Vd.ap()[:, :, blk * 4:(blk + 1) * 4, :].rearrange("h p t e -> p h t e"), TMst[:, bs, :, :, :], "oV%d" % bs, eng="pool", reads=[B["TMst%d" % bs]])
                    if mine:
                        dma(QTd.ap()[:, :, tsl].rearrange("h p t -> p h t"), FMst[:, bs, 8:12, :], "oQ%d" % bs, eng="pool", reads=[B["FMst%d" % bs]])
                        dma(G2d.ap()[:, :, blk * 4:(blk + 1) * 4, :].rearrange("h p t e -> p h t e"), G2st[:, bs, :, :, :], "oG%d" % bs, eng="pool", reads=[B["G2st%d" % bs]])
                        dma(GFd.ap()[:, :, blk * 4:(blk + 1) * 4, :].rearrange("h p t e -> p h t e"), GFst[:, bs, :, :, :], "oF%d" % bs, eng="pool", reads=[B["GFst%d" % bs]])

                loads(0)
                FE1(0)
                loads(1)
                FE2(0)
                FE1(1)
                loads(2)
                for blk in range(NB):
                    for ti in range(4):
                        if blk + 1 < NB:
                            FE2_tile(blk + 1, ti)
                        TM_tile(blk, ti)
                    items = FE1_items(blk + 2) if blk + 2 < NB else []
                    FM(blk, items)
                    if blk + 3 < NB:
                        loads(blk + 3)
                R.emit()
                B.clear()
            if dbg == "A":
                return nc

            with contextlib.ExitStack() as M:
                mixT = sbt(M, "mixT", [128, 8, NQ], BF16)
                with contextlib.ExitStack() as P:
                    KT2 = sbt(P, "KT", [128, 2, NKV], BF16)
                    QT2 = sbt(P, "QT", [128, 2, NQ], BF16)
                    Vh2 = sbt(P, "Vh", [128, 2, NKV // 128, 129], BF16)
                    G2h2 = sbt(P, "G2h", [128, 2, NQ // 128, 128], BF16)
                    PT = sbt(P, "PT", [128, 3, 2, 512], BF16)
                    stg = sbt(P, "stg", [128, 8, 129], F32)
                    r8 = sbt(P, "r8", [128, 8], F32)
                    o4 = sbt(P, "o4", [128, 4, 128], F32)
                    sq4 = sbt(P, "sq4", [128, 4, 128], F32)
                    ss4 = sbt(P, "ss4", [128, 3, 4], F32)
                    attb = sbt(P, "attb", [128, 4, 128], BF16)
                    NJ = NKV // 128
                    NM = NQ // 512

                    def near_info(m, j):
                        if ji == 0 or j <= 15:
                            tau = j - 4 * m
                            if -1 <= tau <= 4:
                                i0, i1 = max(0, tau - 1), min(3, tau + 1)
                                base = 0 if tau <= 1 else 384
                                return (64 if tau <= 1 else 65), (base + (1 - tau + i0) * 128, i0, i1 - i0 + 1)
                            return (64 if tau < -1 else 65), None
                        if j == 16 and m == 3:
                            return 16, (768, 3, 1)
                        if j == 63 and m == 0:
                            return 63, (896, 0, 1)
                        return j, None

                    step_id = 0
                    pend = [None]
                    def loadsB(hh):
                        hp = hh % 2
                        dma(QT2[:, hp, 0:NQ // 2], QTd.ap()[hh, :, 0:NQ // 2], "lQ%d_0" % hp, writes=[B["QT%d_0" % hp]])
                        for ci in range(4):
                            k0, k1 = ci * (NKV // 4), (ci + 1) * (NKV // 4)
                            dma(KT2[:, hp, k0:k1], KTd.ap()[hh, :, k0:k1], "lK%d_%d" % (hp, ci), writes=[B["KT%d_%d" % (hp, ci)]])
                            dma(Vh2[:, hp, k0 // 128:k1 // 128, :], Vd.ap()[hh, :, k0 // 128:k1 // 128, :], "lV%d_%d" % (hp, ci), writes=[B["Vh%d_%d" % (hp, ci)]])
                        dma(QT2[:, hp, NQ // 2:NQ], QTd.ap()[hh, :, NQ // 2:NQ], "lQ%d_1" % hp, writes=[B["QT%d_1" % hp]])
                        dma(G2h2[:, hp, :, :], G2d.ap()[hh, :, 0:NQ // 128, :], "lG%d" % hp, writes=[B["G2h%d" % hp]])

                    loadsB(0)
                    for hh in range(4):
                        hp = hh % 2
                        if pend[0] is not None:
                            for it in pend[0]:
                                it[0]()
                            pend[0] = None
                        if hh + 1 < 4:
                            loadsB(hh + 1)
                        KT = KT2[:, hp, :]
                        QT = QT2[:, hp, :]
                        Vh = Vh2[:, hp, :, :]
                        G2h = G2h2[:, hp, :, :]
                        bG2h = B["G2h%d" % hp]
                        JQ = NJ // 4
                        steps = [(m, j) for m in range(NM) for j in range(NJ)]

                        def QK(si, m, j):
                            slot = si % 2
                            col, off = near_info(m, j)
                            for c in range(2):
                                R.op("pe", lambda h, c=c, j=j, m=m, slot=slot, off=off, KT=KT, QT=QT: h.matmul(ps[:, slot * 2 + c, :], lhsT=KT[64 * c:64 * c + 64, j * 128:(j + 1) * 128],
                                                                                                 rhs=QT[64 * c:64 * c + 64, m * 512:(m + 1) * 512], start=True, stop=(off is None)),
                                     reads=[B["KT%d_%d" % (hp, j // JQ)], B["QT%d_%d" % (hp, (2 * m) // NM)]], writes=[B["S%d" % slot]])
                            if off is not None:
                                for c in range(2):
                                    R.op("pe", lambda h, c=c, slot=slot, off=off, hh=hh: h.matmul(ps[:, slot * 2 + c, off[1] * 128:(off[1] + off[2]) * 128], lhsT=ident[:, :],
                                                                                                  rhs=BTw[:, hh, off[0]:off[0] + off[2] * 128], start=False, stop=True),
                                         reads=[], writes=[B["S%d" % slot]])
                            return col

                        cols = {}
                        cols[0] = QK(step_id, *steps[0])
                        for k, (m, j) in enumerate(steps):
                            si = step_id + k
                            if k + 1 < len(steps):
                                cols[k + 1] = QK(si + 1, *steps[k + 1])
                            slot = si % 2
                            p3 = si % 3
                            col = cols.pop(k)
                            R.op("act", lambda h, slot=slot, p3=p3, col=col, hh=hh: h.activation(out=PT[:, p3, :, :], in_=ps[:, slot * 2:slot * 2 + 2, :], func=AF.Exp,
                                                                                                 bias=cb[:, hh, col:col + 1], scale=0.125),
                                 reads=[B["S%d" % slot]], writes=[B["PT%d" % p3]])
                            for c in range(2):
                                for i in range(4):
                                    idx = c * 4 + i
                                    R.op("pe", lambda h, p3=p3, c=c, i=i, idx=idx, j=j, Vh=Vh: h.matmul(ps[:, 4 + idx // 3, (idx % 3) * 129:(idx % 3) * 129 + 129],
                                                                                               lhsT=PT[:, p3, c, i * 128:(i + 1) * 128], rhs=Vh[:, j, :],
                                                                                               start=(j == 0 and idx % 3 == 0), stop=(j == NJ - 1),
                                                                                               skip_group_check=True),
                                         reads=[B["PT%d" % p3], B["Vh%d_%d" % (hp, j // JQ)]], writes=[B["acc%d" % (4 + idx // 3)]])
                            if j == NJ - 1:
                                for bk, (a0, n) in enumerate([(0, 3), (3, 3), (6, 2)]):
                                    R.op("dve", lambda h, bk=bk, a0=a0, n=n: h.tensor_copy(out=stg[:, a0:a0 + n, :],
                                                                                          in_=ps[:, 4 + bk, 0:n * 129].rearrange("p (a b) -> p a b", b=129)),
                                         reads=[B["acc%d" % (4 + bk)]], writes=[B["stg%d" % bk]])
                                R.op("dve", lambda h: h.reciprocal(out=r8[:, :], in_=stg[:, :, 128]), reads=[B["stg0"], B["stg1"], B["stg2"]], writes=[B["r8"]])
                                R.op("dve", lambda h: h.tensor_scalar(out=r8[:, 4:8], in0=r8[:, 4:8], scalar1=lamc[:, 3:4], scalar2=None, op0=ALU.mult),
                                     reads=[B["r8"]], writes=[B["r8"]])
                                R.op("dve", lambda h: h.tensor_tensor(out=stg[:, :, 0:128], in0=stg[:, :, 0:128], in1=r8[:, :].unsqueeze(2).to_broadcast([128, 8, 128]), op=ALU.mult),
                                     reads=[B["r8"], B["stg0"], B["stg1"], B["stg2"]], writes=[B["stg0"], B["stg1"], B["stg2"]])
                                R.op("dve", lambda h: h.tensor_tensor(out=o4[:, :, :], in0=stg[:, 0:4, 0:128], in1=stg[:, 4:8, 0:128], op=ALU.add),
                                     reads=[B["stg0"], B["stg1"], B["stg2"]], writes=[B["o4"]])
                                R.op("dve", lambda h: h.tensor_tensor(out=sq4[:, :, :], in0=o4[:, :, :], in1=o4[:, :, :], op=ALU.mult),
                                     reads=[B["o4"]], writes=[B["sq4"]])
                                R.op("dve", lambda h: h.tensor_reduce(out=ss4[:, 0, :], in_=sq4[:, :, :], axis=AX.X, op=ALU.add),
                                     reads=[B["sq4"]], writes=[B["ss4"]])
                                def _tail1(m=m, G2h=G2h, bG2h=bG2h, hh=hh):
                                    rstd_ops(ss4[:, 0, :], ss4[:, 2, :], 1.0 / 128, epsc[:, 1:2], "rB", [B["ss4"]], B["rs4"], ss4[:, 1, :])
                                    R.op("dve", lambda h: h.tensor_tensor(out=o4[:, :, :], in0=o4[:, :, :], in1=ss4[:, 2, :].unsqueeze(2).to_broadcast([128, 4, 128]), op=ALU.mult),
                                         reads=[B["rs4"], B["o4"]], writes=[B["o4"]])
                                    R.op("dve", lambda h, m=m, G2h=G2h: h.tensor_tensor(out=attb[:, :, :], in0=o4[:, :, :], in1=G2h[:, m * 4:(m + 1) * 4, :], op=ALU.mult),
                                         reads=[B["o4"], bG2h], writes=[B["attb"]])

                                def _tail2(m=m, hh=hh):
                                    for i in range(4):
                                        R.op("pe", lambda h, i=i: h.transpose(out=psb(7)[:, i * 128:(i + 1) * 128], in_=attb[:, i, :], identity=ident[:, :]),
                                             reads=[B["attb"]], writes=[B["pb7"]])
                                    R.op("dve", lambda h, hh=hh, m=m: h.tensor_copy(out=mixT[:, hh, m * 512:(m + 1) * 512], in_=psb(7)[:, 0:512]),
                                         reads=[B["pb7"]], writes=[B["mixT"]])
                                pend[0] = [[_tail1, 5], [_tail2, 12]]
                            elif pend[0] is not None:
                                for it in pend[0]:
                                    it[1] -= 1
                                while pend[0] and pend[0][0][1] <= 0:
                                    pend[0].pop(0)[0]()
                                if not pend[0]:
                                    pend[0] = None
                        step_id += len(steps)
                    if pend[0] is not None:
                        for it in pend[0]:
                            it[0]()
                        pend[0] = None
                    R.emit()
                    B.clear()
                if dbg == "B":
                    dma(dmix.ap()[:, 0:8 * NQ], mixT[:, :, :].rearrange("p a b -> p (a b)"), "g4")
                    R.emit()
                    return nc

                with contextlib.ExitStack() as P:
                    uT = sbt(P, "uT", [128, 2, NKV], BF16)
                    GFg = sbt(P, "GFg", [128, 2, NQ // 128, 128], BF16)
                    PQ = sbt(P, "PQ", [128, 2, 128, N2], BF16)
                    mt = sbt(P, "mt", [128, 2, 2, 4, 2, 128], BF16)
                    X2 = sbt(P, "X2", [128, 2, 4, 2, 128], BF16)
                    Ab = sbt(P, "Ab", [128, 2, 4, 2, 128], BF16)
                    fng = sbt(P, "fng", [128, NK2, 128], BF16)
                    NU = 128 // RR
                    CPB = 512 // NK2
                    def loadsC(g):
                        dma(uT[:, g % 2, :], UTd.ap()[g, :, 0:NKV], "lU%d" % (g % 2), writes=[B["uT%d" % (g % 2)]])
                        dma(GFg[:, g % 2, :, :], GFd.ap()[g, :, 0:NQ // 128, :], "lF%d" % (g % 2), writes=[B["GFg%d" % (g % 2)]])

                    loadsC(0)
                    for g in range(4):
                        gp2 = g % 2
                        if g + 1 < 4:
                            loadsC(g + 1)
                        for s2 in range(N2):
                            bnk = (0, 1, 6, 7)[(s2 // 2) % 4]
                            R.op("pe", lambda h, s2=s2, bnk=bnk, g=g: h.matmul(ps[:, bnk, (s2 % 2) * 256:(s2 % 2) * 256 + 256], lhsT=uT[:, g % 2, s2:NKV:N2], rhs=CSW[:, g, :],
                                                                              start=True, stop=True),
                                 reads=[B["uT%d" % (g % 2)]], writes=[B["pb%d" % bnk]])
                            if s2 % 2 == 1:
                                eng = ("act", "dve")[(s2 // 2) % 2]
                                src = ps[:, bnk, :].rearrange("p (s q d) -> p q d s", s=2, q=2)
                                dst = PQ[:, :, :, s2 - 1:s2 + 1]
                                if eng == "act":
                                    R.op("act", lambda h, src=src, dst=dst: h.activation(out=dst, in_=src, func=AF.Copy), reads=[B["pb%d" % bnk]], writes=[B["PQ"]])
                                else:
                                    R.op("dve", lambda h, src=src, dst=dst: h.tensor_copy(out=dst, in_=src), reads=[B["pb%d" % bnk]], writes=[B["PQ"]])
                        NXS = 3 if RR == 2 else 2
                        XB0 = (2, 4, 0)
                        xbufs_of = lambda xs: [B["pb0"], B["pb1"]] if xs == 2 else [B["X%d" % xs]]

                        def s1(ub):
                            sl = ub % NXS
                            b0 = XB0[sl]
                            xbufs = xbufs_of(sl)
                            for ui in range(4):
                                u = ub * 4 + ui
                                dstp = ps[:, b0 + ui // 2, (ui % 2) * 256:(ui % 2) * 256 + 256]
                                R.op("pe", lambda h, u=u, dstp=dstp: h.matmul(dstp, lhsT=PQ[:, 0, u * RR:(u + 1) * RR, :].rearrange("p d s -> p (d s)"), rhs=F1ab[:, 0:256],
                                                                              start=True, stop=False),
                                     reads=[B["PQ"]], writes=xbufs)
                                R.op("pe", lambda h, u=u, dstp=dstp: h.matmul(dstp, lhsT=PQ[:, 1, u * RR:(u + 1) * RR, :].rearrange("p d s -> p (d s)"), rhs=F1ab[:, 256:512],
                                                                              start=False, stop=True),
                                     reads=[B["PQ"]], writes=xbufs)
                        def s2(ub):
                            xs = ub % NXS
                            b0 = XB0[xs]
                            sl = ub % 2
                            X = ps[:, b0:b0 + 2, :].rearrange("p b (u a k) -> p (b u) a k", u=2, a=2)
                            R.op("act", lambda h, sl=sl, X=X: h.activation(out=X2[:, sl, :, :, :], in_=X, func=AF.Copy), reads=xbufs_of(xs), writes=[B["X2%d" % sl]])
                            for mi in range(2):
                                R.op("dve", lambda h, sl=sl, mi=mi: h.tensor_tensor(out=mt[:, sl, mi, :, :, :], in0=X2[:, sl, :, :, :],
                                                                                   in1=twb[ji][:, mi, :, :].unsqueeze(1).to_broadcast([128, 4, 2, 128]), op=ALU.mult),
                                     reads=[B["X2%d" % sl]], writes=[B["mt%d%d" % (sl, mi)]])
                            for bi in range(2):
                                R.op("dve", lambda h, sl=sl, bi=bi: h.tensor_tensor(out=Ab[:, sl, :, bi, :], in0=mt[:, sl, bi, :, 0, :], in1=mt[:, sl, bi, :, 1, :], op=ALU.add),
                                     reads=[B["mt%d%d" % (sl, bi)]], writes=[B["Ab%d" % sl]])
                            YB = [6, 7, 0, 1]
                            UPB = 512 // NK2
                            for ui in range(4):
                                u = ub * 4 + ui
                                ycol = (u % UPB) * NK2
                                for r in range(RR):
                                    yb = YB[r]
                                    for bi in range(2):
                                        R.op("pe", lambda h, sl=sl, ui=ui, r=r, bi=bi, yb=yb, ycol=ycol: h.matmul(
                                            ps[:, yb, ycol:ycol + NK2], lhsT=Ab[r * N2:(r + 1) * N2, sl, ui, bi, :], rhs=F2[ji][r * N2:(r + 1) * N2, bi, 0:NK2],
                                            start=(bi == 0), stop=(bi == 1), tile_position=(r * N2, 0), skip_group_check=True),
                                             reads=[B["Ab%d" % sl]], writes=[B["pb%d" % yb]])
                                if u % UPB == UPB - 1:
                                    u0 = u - (UPB - 1)
                                    for r in range(RR):
                                        yb = YB[r]
                                        csl = slice(u0 * RR + r, (u0 + UPB) * RR, RR)
                                        R.op("dve", lambda h, yb=yb, csl=csl, gp2=gp2: h.tensor_tensor(out=fng[:, :, csl],
                                                                                            in0=ps[:, yb, :].rearrange("p (c k) -> p k c", k=NK2),
                                                                                            in1=GFg[:, gp2, 0:NK2, csl], op=ALU.mult),
                                             reads=[B["pb%d" % yb], B["GFg%d" % gp2]], writes=[B["fng"]])
                        LA = NXS - 1
                        for ub in range(min(LA, NU // 4)):
                            s1(ub)
                        for ub in range(NU // 4):
                            if ub + LA < NU // 4:
                                s1(ub + LA)
                            s2(ub)
                        for t4 in range(NK2 // 4):
                            tb = (0, 1)[t4 % 2]
                            for i in range(4):
                                R.op("pe", lambda h, t4=t4, i=i, tb=tb: h.transpose(out=psb(tb)[:, i * 128:(i + 1) * 128], in_=fng[:, t4 * 4 + i, :], identity=ident[:, :]),
                                     reads=[B["fng"]], writes=[B["pb%d" % tb]])
                            if t4 % 2 == 0:
                                R.op("act", lambda h, g=g, t4=t4, tb=tb: h.activation(out=mixT[:, 4 + g, t4 * 512:(t4 + 1) * 512], in_=psb(tb)[:, 0:512], func=AF.Copy),
                                     reads=[B["pb%d" % tb]], writes=[B["mixT"]])
                            else:
                                R.op("dve", lambda h, g=g, t4=t4, tb=tb: h.tensor_copy(out=mixT[:, 4 + g, t4 * 512:(t4 + 1) * 512], in_=psb(tb)[:, 0:512]),
                                     reads=[B["pb%d" % tb]], writes=[B["mixT"]])
                    R.emit()
                    B.clear()
                if dbg == "C":
                    dma(dmix.ap()[:, 0:8 * NQ], mixT[:, :, :].rearrange("p a b -> p (a b)"), "g4")
                    R.emit()
                    return nc

                with contextlib.ExitStack() as P:
                    wo_b = sbt(P, "wo_b", [128, 8, D], BF16)
                    wg_b = sbt(P, "wg_b", [128, 8, D], BF16)
                    wp_b = sbt(P, "wp_b", [128, 2, D], BF16)
                    load_weight(None, wo_b, wout.ap(), 8, D, "wo")
                    load_weight(None, wg_b, wgate.ap(), 8, D, "wg")
                    load_weight(None, wp_b, wproj.ap(), 2, D, "wp")
                    xt = sbt(P, "xtD", [128, 2, D], F32)
                    ptl = sbt(P, "ptl", [128, 2, 256], F32)
                    pbf = sbt(P, "pbf", [128, 2, 256], BF16)
                    pT = sbt(P, "pT", [128, 2, 2, 128], BF16)
                    h1 = sbt(P, "h1", [128, 2, D], F32)
                    junk = sbt(P, "junkD", [128, D], BF16)
                    ssD = sbt(P, "ssD", [128, 2, 6], F32)
                    hnb = sbt(P, "hnb", [128, 2, D], BF16)
                    hnT = sbt(P, "hnT", [128, 2, 8, 128], BF16)
                    sg = sbt(P, "sg", [128, 2, D], F32)
                    gp = sbt(P, "gp", [128, 2, D], F32)
                    NT = NQ // 128

                    def stA1(t):
                        s = t % 2
                        dma(xt[:, s, :], x_ap[t * 128:(t + 1) * 128, :], "dx%d" % s, writes=[B["xt%d" % s]])
                        dma(ptl[:, s, :], pj[ji].ap()[t * 128:(t + 1) * 128, :], "dp%d" % s, writes=[B["ptl%d" % s]])
                        R.op("dve", lambda h, s=s: h.memset(ssD[:, s, 0:1], 0.0), writes=[B["ssa%d" % s]])
                        R.op("dve", lambda h, s=s: h.memset(ssD[:, s, 3:4], 0.0), writes=[B["ssb%d" % s]])
                        for half in range(2):
                            for c in range(8):
                                R.op("pe", lambda h, t=t, c=c, half=half: h.matmul(ps[:, half, :], lhsT=mixT[:, c, t * 128:(t + 1) * 128], rhs=wo_b[:, c, half * 512:(half + 1) * 512],
                                                                                   start=(c == 0), stop=(c == 7)),
                                     reads=[B["wo_%d" % c]], writes=[B["pb01"]])
                        R.op("dve", lambda h, s=s: h.tensor_tensor(out=h1[:, s, :], in0=ps[:, 0:2, :].rearrange("p a b -> p (a b)"), in1=xt[:, s, :], op=ALU.add),
                             reads=[B["pb01"], B["xt%d" % s]], writes=[B["h1%d" % s]])
                        R.op("act", lambda h, s=s: h.activation(out=junk[:, :], in_=h1[:, s, :], func=AF.Square, accum_out=ssD[:, s, 0:1]),
                             reads=[B["h1%d" % s]], writes=[B["junk"], B["ssa%d" % s]])
                        rstd_ops(ssD[:, s, 0:1], ssD[:, s, 2:3], 1.0 / D, epsc[:, 0:1], "rD%d" % s, [B["ssa%d" % s]], B["rsa%d" % s], ssD[:, s, 1:2])
                        R.op("dve", lambda h, s=s: h.scalar_tensor_tensor(out=hnb[:, s, :], in0=h1[:, s, :], scalar=ssD[:, s, 2:3], in1=gvec[:, 1, :], op0=ALU.mult, op1=ALU.mult),
                             reads=[B["h1%d" % s], B["rsa%d" % s]], writes=[B["hnb%d" % s]])
                        R.op("dve", lambda h, s=s: h.tensor_copy(out=pbf[:, s, :], in_=ptl[:, s, :]), reads=[B["ptl%d" % s]], writes=[B["pbf%d" % s]])

                    def stA2(t):
                        s = t % 2
                        for c in range(8):
                            R.op("pe", lambda h, s=s, c=c: h.transpose(out=psb(2)[:, c * 128:(c + 1) * 128], in_=hnb[:, s, c * 128:(c + 1) * 128], identity=ident[:, :]),
                                 reads=[B["hnb%d" % s]], writes=[B["pb2"]])
                        R.op("act", lambda h, s=s: h.activation(out=hnT[:, s, :, :], in_=psb(2).rearrange("p (c t) -> p c t", c=8), func=AF.Copy),
                             reads=[B["pb2"]], writes=[B["hnT%d" % s]])
                        for c in range(2):
                            R.op("pe", lambda h, s=s, c=c: h.transpose(out=psb(5)[:, c * 128:(c + 1) * 128], in_=pbf[:, s, c * 128:(c + 1) * 128], identity=ident[:, :]),
                                 reads=[B["pbf%d" % s]], writes=[B["pb5"]])
                        R.op("dve", lambda h, s=s: h.tensor_copy(out=pT[:, s, :, :], in_=psb(5)[:, 0:256].rearrange("p (c t) -> p c t", c=2)),
                             reads=[B["pb5"]], writes=[B["pT%d" % s]])

                    def stB(t):
                        s = t % 2
                        for half in range(2):
                            for c in range(8):
                                R.op("pe", lambda h, s=s, c=c, half=half: h.matmul(ps[:, 3 + half, :], lhsT=hnT[:, s, c, :], rhs=wg_b[:, c, half * 512:(half + 1) * 512],
                                                                                   start=(c == 0), stop=(c == 7)),
                                     reads=[B["hnT%d" % s], B["wg_%d" % c]], writes=[B["pb34"]])
                        R.op("act", lambda h, s=s: h.activation(out=sg[:, s, :], in_=ps[:, 3:5, :].rearrange("p a b -> p (a b)"), func=AF.Sigmoid),
                             reads=[B["pb34"]], writes=[B["sg%d" % s]])
                        for half in range(2):
                            for c in range(2):
                                R.op("pe", lambda h, s=s, c=c, half=half: h.matmul(ps[:, 6 + half, :], lhsT=pT[:, s, c, :], rhs=wp_b[:, c, half * 512:(half + 1) * 512],
                                                                                   start=(c == 0), stop=(c == 1)),
                                     reads=[B["pT%d" % s], B["wp_%d" % c]], writes=[B["pb67"]])
                        R.op("dve", lambda h, s=s: h.tensor_tensor(out=gp[:, s, :], in0=ps[:, 6:8, :].rearrange("p a b -> p (a b)"), in1=sg[:, s, :], op=ALU.mult),
                             reads=[B["pb67"], B["sg%d" % s]], writes=[B["gp%d" % s]])
                        R.op("dve", lambda h, s=s: h.tensor_tensor(out=gp[:, s, :], in0=gp[:, s, :], in1=h1[:, s, :], op=ALU.add),
                             reads=[B["gp%d" % s], B["h1%d" % s]], writes=[B["gp%d" % s]])
                        R.op("act", lambda h, s=s: h.activation(out=junk[:, :], in_=gp[:, s, :], func=AF.Square, accum_out=ssD[:, s, 3:4]),
                             reads=[B["gp%d" % s]], writes=[B["junk"], B["ssb%d" % s]])
                        rstd_ops(ssD[:, s, 3:4], ssD[:, s, 5:6], 1.0 / D, epsc[:, 0:1], "rE%d" % s, [B["ssb%d" % s]], B["rsb%d" % s], ssD[:, s, 4:5])
                        R.op("dve", lambda h, s=s: h.scalar_tensor_tensor(out=sg[:, s, :], in0=gp[:, s, :], scalar=ssD[:, s, 5:6], in1=gvec[:, 2, :], op0=ALU.mult, op1=ALU.mult),
                             reads=[B["gp%d" % s], B["rsb%d" % s]], writes=[B["sg%d" % s]])
                        dma(yj[ji].ap()[t * 128:(t + 1) * 128, :], sg[:, s, :], "y%d" % s, eng="pool", reads=[B["sg%d" % s]])

                    stA1(0)
                    stA2(0)
                    for t in range(NT):
                        if t + 1 < NT:
                            stA1(t + 1)
                        stB(t)
                        if t + 1 < NT:
                            stA2(t + 1)
                    R.emit()
                    B.clear()
    return nc


def _rel_bucket_table():
    rel = np.arange(-639, 640, dtype=np.int32)
    ret = np.where(rel > 0, 16, 0)
    n = np.abs(rel)
    nf = np.maximum(n, 1).astype(np.float32)
    large = 8 + (np.log(nf / np.float32(8)) / np.float32(math.log(128 / 8)) * np.float32(8)).astype(np.int32)
    large = np.minimum(large, 15)
    return ret + np.where(n < 8, n, large)


def _host_consts():
    bf = ml_dtypes.bfloat16
    c = {}
    c["ident"] = np.eye(128, dtype=np.float32).astype(bf)
    c["Jm"] = np.ascontiguousarray(np.eye(128, dtype=np.float32)[::-1])
    i = np.arange(128)
    ang = 2 * np.pi * np.outer(i, i) / 128.0
    Cm, Sm = np.cos(ang), np.sin(ang)
    c["CS"] = (np.concatenate([Cm, Sm], 1) / np.sqrt(128.0)).astype(np.float32)
    c["F1ab"] = np.concatenate([Cm, Sm, -Sm, Cm], 1).astype(np.float32).astype(bf)
    bucket = _rel_bucket_table()
    per_core = []
    for core in range(NCORES):
        r = core % 4
        d = {}
        for ji, job in enumerate(JOBS):
            N = job["NKV"]; N2 = job["N2"]; RR = job["R"]; NK2 = job["NK2"]
            off = 0 if ji == 0 else 2048 * r
            s2 = np.arange(N2)[:, None].astype(np.float64)
            k1 = np.arange(128)[None, :].astype(np.float64)
            th = 2 * np.pi * (((s2 + off) * (k1 + off)) % N) / N
            twr, twi = np.cos(th), -np.sin(th)
            T = np.stack([twr, twi, -twr], 1)
            d["tw%d" % ji] = np.ascontiguousarray(np.tile(T, (RR, 1, 1))).astype(np.float32)
            k2 = np.arange(NK2)[None, :].astype(np.float64)
            ph = 2 * np.pi * ((s2 * k2) % N2) / N2
            F = np.zeros((N2, 2, 32), np.float64)
            F[:, 0, :NK2] = np.cos(ph) / np.sqrt(N)
            F[:, 1, :NK2] = np.sin(ph) / np.sqrt(N)
            d["F2%d" % ji] = np.ascontiguousarray(np.tile(F, (RR, 1, 1))).astype(np.float32).astype(bf)
        OHG = np.zeros((32, 3, 1280), np.float32)
        tp = np.arange(1279)
        rel = 639 - tp
        OHG[bucket[rel + 639], 0, tp] = 8.0
        if r < 3:
            OHG[:, 1, :] = OHG[:, 0, :]
        else:
            OHG[15, 1, :1279] = 8.0
        if r > 0:
            OHG[:, 2, :] = OHG[:, 0, :]
        else:
            OHG[31, 2, :1279] = 8.0
        d["OHG"] = OHG
        OHC = np.zeros((32, NCOL), np.float32)
        for j in range(16, 64):
            OHC[31 if j < 64 - 16 * r else 15, j] = 1.0
        OHC[15, 64] = 1.0
        OHC[31, 65] = 1.0
        d["OHC"] = OHC
        per_core.append(d)
    return c, per_core


_CACHE = {}


def kernel(x_prompt, x_sample, p_prompt, p_sample, norm_mix_g, w_in, lambda_params, subln_g, rel_bias,
           w_fourier, w_out, ple_norm_g, w_ple_gate, w_ple_proj, final_norm_g):
    f = lambda a: np.ascontiguousarray(np.asarray(a, dtype=np.float32))
    x_prompt, x_sample, p_prompt, p_sample = f(x_prompt), f(x_sample), f(p_prompt), f(p_sample)
    if "nc" not in _CACHE:
        _CACHE["nc"] = build_program()
        _CACHE["consts"] = _host_consts()
    nc = _CACHE["nc"]
    cglob, cper = _CACHE["consts"]
    shared = dict(win=f(w_in)[0], wout=f(w_out)[0], wgate=f(w_ple_gate)[0], wproj=f(w_ple_proj)[0], wf=f(w_fourier)[0],
                  gmix=f(norm_mix_g).reshape(1, D), gple=f(ple_norm_g).reshape(1, D), gfin=f(final_norm_g).reshape(1, D),
                  gsub=f(subln_g).reshape(1, 128), lamp=f(lambda_params).reshape(1, 256), relb=f(rel_bias))
    shared.update(cglob)
    in_maps = []
    for core in range(NCORES):
        b, r = core // 4, core % 4
        m = dict(shared)
        m.update(cper[core])
        m["xp"] = x_prompt[core]
        m["pp"] = p_prompt[0, core]
        m["xs"] = np.ascontiguousarray(np.roll(x_sample[b], -2048 * r, axis=0))
        m["psm"] = np.ascontiguousarray(p_sample[0, b, 2048 * r:2048 * (r + 1)])
        in_maps.append(m)
    if _CACHE.get("prep_only"):
        return in_maps
    res = run_bass_kernel_spmd(nc, in_maps, core_ids=list(range(NCORES)))
    yp = np.stack([res.results[c]["yp"] for c in range(NCORES)], 0)
    ys = np.zeros((2, 8192, D), np.float32)
    for core in range(NCORES):
        b, r = core // 4, core % 4
        ys[b, 2048 * r:2048 * (r + 1)] = res.results[core]["ys"]
    return (yp.astype(np.float32), ys)
```

```python
import contextlib
import math
import numpy as np
import ml_dtypes
import concourse.bass as bass
import concourse.mybir as mybir
from concourse.bass_utils import run_bass_kernel_spmd

F32 = mybir.dt.float32
BF16 = mybir.dt.bfloat16
AF = mybir.ActivationFunctionType
ALU = mybir.AluOpType
AX = mybir.AxisListType

D = 1024
NCORES = 8
JOBS = [dict(NKV=4096, NQ=4096, N2=32, R=4, NK2=32), dict(NKV=8192, NQ=2048, N2=64, R=2, NK2=16)]
NCOL = 67


class Buf:
    __slots__ = ("w", "r", "const")

    def __init__(self, const=False):
        self.w = {}
        self.r = {}
        self.const = const


def _merge(d, s):
    for k, v in s.items():
        if d.get(k, -1) < v:
            d[k] = v


class Rec:
    ENGS = ("pe", "act", "dve", "pool", "sp")

    def __init__(self, nc, stack):
        self.nc = nc
        self.stack = stack
        self.sems = {e: stack.enter_context(nc.semaphore("s_" + e)) for e in self.ENGS if e != "sp"}
        self.base_cnt = {e: 0 for e in self.ENGS}
        self.dcnt = {}
        self.reset()

    def reset(self):
        self.ops = {e: [] for e in self.ENGS}

    def op(self, eng, fn, reads=(), writes=(), dma_key=None):
        deps = {}
        for b in reads:
            _merge(deps, b.w)
        for b in writes:
            _merge(deps, b.r)
            _merge(deps, b.w)
        lst = self.ops[eng]
        idx = len(lst)
        if dma_key is None:
            tk = {eng: idx}
        else:
            n = self.dcnt.get(dma_key, 0) + 16
            self.dcnt[dma_key] = n
            tk = {"d:" + dma_key: n}
            if ("d:" + dma_key) not in self.sems:
                self.sems["d:" + dma_key] = self.stack.enter_context(self.nc.semaphore("d_" + dma_key))
        lst.append((fn, deps, dma_key))
        for b in reads:
            if not b.const:
                _merge(b.r, tk)
        for b in writes:
            b.w = dict(tk)
            b.r = {}
        return tk

    def emit(self, bufs_to_clear=()):
        nc = self.nc
        needed = {e: set() for e in self.ENGS}
        for e in self.ENGS:
            for (fn, deps, dk) in self.ops[e]:
                for ch, v in deps.items():
                    if ch in needed:
                        if ch == "pe" and e == "pe":
                            continue
                        needed[ch].add(v)
            if e != "sp":
                for i in range(len(self.ops[e]) - 1, -1, -1):
                    if self.ops[e][i][2] is None:
                        needed[e].add(i)
                        break
        cum = {}
        for e in self.ENGS:
            c = self.base_cnt[e]
            arr = []
            nd = needed[e]
            for i in range(len(self.ops[e])):
                if i in nd and self.ops[e][i][2] is None:
                    c += 1
                arr.append(c)
            cum[e] = arr
        final = {e: (cum[e][-1] if cum[e] else self.base_cnt[e]) for e in self.ENGS if e != "sp"}
        dfinal = {"d:" + k: v for k, v in self.dcnt.items()}
        sems = self.sems
        with nc.Block() as block:
            handles = {"pe": block.tensor, "act": block.scalar, "dve": block.vector,
                       "pool": block.gpsimd, "sp": block.sync}

            def run(e):
                def body(h):
                    seen = {}
                    for i, (fn, deps, dk) in enumerate(self.ops[e]):
                        for ch, v in deps.items():
                            if ch in cum:
                                if ch == "pe" and e == "pe":
                                    continue
                                val = cum[ch][v]
                            else:
                                val = v
                            if seen.get(ch, 0) < val:
                                h.wait_ge(sems[ch], val)
                                seen[ch] = val
                        inst = fn(h)
                        if dk is not None:
                            inst.then_inc(sems["d:" + dk], 16)
                        elif i in needed[e]:
                            inst.then_inc(sems[e], 1)
                    for ch, val in list(final.items()) + list(dfinal.items()):
                        if ch == e:
                            continue
                        if val > 0 and seen.get(ch, 0) < val:
                            h.wait_ge(sems[ch], val)
                return body

            for e in self.ENGS:
                handles[e](run(e))
        for e in final:
            self.base_cnt[e] = final[e]
        self.reset()
        for b in bufs_to_clear:
            b.w = {}
            b.r = {}


class BufSet:
    def __init__(self):
        self.d = {}

    def __getitem__(self, k):
        b = self.d.get(k)
        if b is None:
            b = Buf()
            self.d[k] = b
        return b

    def clear(self):
        self.d = {}


def build_program(dbg=None, njobs=2):
    nc = bass.Bass("TRN2", target_bir_lowering=False)
    dt_in = lambda n, s, d=F32: nc.dram_tensor(n, list(s), d, kind="ExternalInput")
    xj = [dt_in("xp", [4096, D]), dt_in("xs", [8192, D])]
    pj = [dt_in("pp", [4096, 256]), dt_in("psm", [2048, 256])]
    yj = [nc.dram_tensor("yp", [4096, D], F32, kind="ExternalOutput"),
          nc.dram_tensor("ys", [2048, D], F32, kind="ExternalOutput")]
    win = dt_in("win", [D, 3072]); wout = dt_in("wout", [D, D]); wgate = dt_in("wgate", [D, D])
    wproj = dt_in("wproj", [256, D]); wf = dt_in("wf", [4, 128, 128])
    gmix_d = dt_in("gmix", [1, D]); gple_d = dt_in("gple", [1, D]); gfin_d = dt_in("gfin", [1, D])
    gsub_d = dt_in("gsub", [1, 128]); lam_d = dt_in("lamp", [1, 256]); relb_d = dt_in("relb", [32, 4])
    ident_d = dt_in("ident", [128, 128], BF16); J_d = dt_in("Jm", [128, 128]); CS_d = dt_in("CS", [128, 256])
    F1_d = dt_in("F1ab", [128, 512], BF16)
    tw_d = [dt_in("tw0", [128, 3, 128]), dt_in("tw1", [128, 3, 128])]
    F2_d = [dt_in("F20", [128, 2, 32], BF16), dt_in("F21", [128, 2, 32], BF16)]
    OHG_d = dt_in("OHG", [32, 3, 1280]); OHC_d = dt_in("OHC", [32, NCOL])
    kw = dict(kind="ExternalOutput") if dbg else {}
    Gd = nc.dram_tensor("Gd", [3, 4, 1280], F32, **kw)
    KTd = nc.dram_tensor("KTd", [4, 128, 8192], BF16, **kw)
    QTd = nc.dram_tensor("QTd", [4, 128, 4096], BF16, **kw)
    UTd = nc.dram_tensor("UTd", [4, 128, 8192], BF16, **kw)
    Vd = nc.dram_tensor("Vd", [4, 128, 64, 129], BF16, **kw)
    G2d = nc.dram_tensor("G2d", [4, 128, 32, 128], BF16, **kw)
    GFd = nc.dram_tensor("GFd", [4, 128, 32, 128], BF16, **kw)
    if dbg:
        dcb = nc.dram_tensor("dcb", [128, 4 * NCOL], F32, kind="ExternalOutput")
        dbt = nc.dram_tensor("dbt", [128, 4 * 1024], BF16, kind="ExternalOutput")
        dcsw = nc.dram_tensor("dcsw", [128, 4 * 256], BF16, kind="ExternalOutput")
        dlam = nc.dram_tensor("dlam", [128, 4], F32, kind="ExternalOutput")
        dmix = nc.dram_tensor("dmix", [128, 8 * 4096], BF16, kind="ExternalOutput")

    with contextlib.ExitStack() as G:
        _cnt = [0]

        def sbt(st, n, s, d):
            _cnt[0] += 1
            return st.enter_context(nc.sbuf_tensor("sb%d_%s" % (_cnt[0], n), list(s), d))
        R = Rec(nc, G)
        B = BufSet()
        ps = G.enter_context(nc.psum_tensor("ps", [128, 8, 512], F32))

        def psb(bank):
            return ps[:, bank, :].bitcast(BF16)

        ident = sbt(G, "ident", [128, 128], BF16)
        Jf = sbt(G, "Jf", [128, 128], F32)
        F1ab = sbt(G, "F1ab", [128, 512], BF16)
        tw = [sbt(G, "tw0", [128, 3, 128], F32), sbt(G, "tw1", [128, 3, 128], F32)]
        twb = [sbt(G, "twb0", [128, 2, 2, 128], BF16), sbt(G, "twb1", [128, 2, 2, 128], BF16)]
        F2 = [sbt(G, "F20", [128, 2, 32], BF16), sbt(G, "F21", [128, 2, 32], BF16)]
        cb = sbt(G, "cb", [128, 4, NCOL], F32)
        lamc = sbt(G, "lamc", [128, 4], F32)
        gsub2 = sbt(G, "gsub2", [128, 128], F32)
        BTw = sbt(G, "BTw", [128, 4, 1024], BF16)
        cb8 = sbt(G, "cb8", [128, 4, NCOL], F32)
        CSW = sbt(G, "CSW", [128, 4, 256], BF16)
        gvec = sbt(G, "gvec", [128, 3, D], F32)
        epsc = sbt(G, "epsc", [128, 2], F32)

        def dma(out, in_, key, reads=(), writes=(), eng="sp"):
            return R.op(eng, lambda h: h.dma_start(out=out, in_=in_), reads=reads, writes=writes, dma_key=key)

        with contextlib.ExitStack() as S:
            relb = sbt(S, "relb", [32, 4], F32)
            OHC = sbt(S, "OHC", [32, NCOL], F32)
            OHG = sbt(S, "OHG", [32, 3, 1280], F32)
            rhs4 = sbt(S, "rhs4", [32, 4, NCOL], F32)
            ones32 = sbt(S, "ones32", [32, 128], F32)
            Gsb = sbt(S, "Gsb", [4, 3, 1280], F32)
            Hsb4 = sbt(S, "Hsb", [128, 4, 2176], F32)
            CS = sbt(S, "CS", [128, 256], F32)
            wfs = sbt(S, "wfs", [128, 4, 128], F32)
            lpb = sbt(S, "lpb", [128, 256], F32)
            lpt = sbt(S, "lpt", [128, 2, 64], F32)
            lps = sbt(S, "lps", [128, 2], F32)
            gsb_ = sbt(S, "gsb_", [128, 128], F32)

            dma(ident[:, :], ident_d.ap()[:, :], "c0", writes=[B["ident"]])
            dma(Jf[:, :], J_d.ap()[:, :], "c1", writes=[B["Jf"]])
            dma(F1ab[:, :], F1_d.ap()[:, :], "c2", writes=[B["F1"]])
            for j in range(2):
                dma(tw[j][:, :, :], tw_d[j].ap()[:, :, :], "c3_%d" % j, writes=[B["tw%d" % j]])
                dma(F2[j][:, :, :], F2_d[j].ap()[:, :, :], "c4_%d" % j, writes=[B["F2%d" % j]])
            dma(relb[:, :], relb_d.ap()[:, :], "c5", writes=[B["relb"]])
            dma(OHC[:, :], OHC_d.ap()[:, :], "c6", writes=[B["OHC"]])
            dma(OHG[:, :, :], OHG_d.ap()[:, :, :], "c7", writes=[B["OHG"]])
            dma(CS[:, :], CS_d.ap()[:, :], "c8", writes=[B["CS"]])
            dma(wfs[:, :, :], wf.ap().rearrange("g c d -> c g d"), "c9", writes=[B["wfs"]])
            dma(lpb[:, :], bass.AP(lam_d, 0, [[0, 128], [1, 256]]), "c10", writes=[B["lpb"]])
            dma(gsb_[:, :], bass.AP(gsub_d, 0, [[0, 128], [1, 128]]), "c11", writes=[B["gsb"]])
            for i, gd in enumerate([gmix_d, gple_d, gfin_d]):
                dma(gvec[:, i, :], bass.AP(gd, 0, [[0, 128], [1, D]]), "c12_%d" % i, writes=[B["gvec%d" % i]])
            for j in range(2):
                R.op("dve", lambda h, j=j: h.tensor_copy(out=twb[j][:, 0, :, :], in_=tw[j][:, 0:2, :]), reads=[B["tw%d" % j]], writes=[B["twb%d" % j]])
                R.op("dve", lambda h, j=j: h.tensor_copy(out=twb[j][:, 1, :, :], in_=tw[j][:, 1:3, :]), reads=[B["tw%d" % j]], writes=[B["twb%d" % j]])
            R.op("pool", lambda h: h.memset(epsc[:, 0:1], 1e-6), writes=[B["eps"]])
            R.op("pool", lambda h: h.memset(epsc[:, 1:2], 1e-5), writes=[B["eps"]])
            R.op("pool", lambda h: h.memset(ones32[:, :], 1.0), writes=[B["ones32"]])
            R.op("pool", lambda h: h.tensor_scalar(out=gsub2[:, :], in0=gsb_[:, :], scalar1=0.8, scalar2=None, op0=ALU.mult),
                 reads=[B["gsb"]], writes=[B["gsub2"]])
            lpv = lpb[:, :].rearrange("p (a b c) -> p a b c", a=2, b=2)
            R.op("dve", lambda h: h.tensor_tensor(out=lpt[:, :, :], in0=lpv[:, :, 0, :], in1=lpv[:, :, 1, :], op=ALU.mult),
                 reads=[B["lpb"]], writes=[B["lpt"]])
            R.op("dve", lambda h: h.tensor_reduce(out=lps[:, :], in_=lpt[:, :, :], axis=AX.X, op=ALU.add),
                 reads=[B["lpt"]], writes=[B["lps"]])
            R.op("act", lambda h: h.activation(out=lamc[:, 0:2], in_=lps[:, :], func=AF.Exp), reads=[B["lps"]], writes=[B["lamc"]])
            R.op("dve", lambda h: h.tensor_tensor(out=lamc[:, 2:3], in0=lamc[:, 0:1], in1=lamc[:, 1:2], op=ALU.subtract),
                 reads=[B["lamc"]], writes=[B["lamc"]])
            R.op("dve", lambda h: h.tensor_scalar(out=lamc[:, 3:4], in0=lamc[:, 2:3], scalar1=0.2, scalar2=-1.0, op0=ALU.add, op1=ALU.mult),
                 reads=[B["lamc"]], writes=[B["lamc"]])
            for hh in range(4):
                R.op("dve", lambda h, hh=hh: h.tensor_scalar(out=rhs4[:, hh, :], in0=OHC[:, :], scalar1=relb[:, hh:hh + 1], scalar2=None, op0=ALU.mult),
                     reads=[B["OHC"], B["relb"]], writes=[B["rhs4"]])
            R.op("pe", lambda h: h.matmul(ps[:, 0, 0:4 * NCOL], lhsT=ones32[:, :], rhs=rhs4[:, :, :].rearrange("p a b -> p (a b)"), start=True, stop=True),
                 reads=[B["ones32"], B["rhs4"]], writes=[B["ps0"]])
            R.op("dve", lambda h: h.tensor_copy(out=cb[:, :, :].rearrange("p a b -> p (a b)"), in_=ps[:, 0, 0:4 * NCOL]), reads=[B["ps0"]], writes=[B["cb"]])
            R.op("dve", lambda h: h.tensor_scalar(out=cb8[:, :, :].rearrange("p a b -> p (a b)"), in0=cb[:, :, :].rearrange("p a b -> p (a b)"),
                                                  scalar1=8.0, scalar2=None, op0=ALU.mult), reads=[B["cb"]], writes=[B["cb8"]])
            bank_of = {}
            bi = 1
            for v in range(3):
                for (c0, c1) in [(0, 512), (512, 1024), (1024, 1280)]:
                    bnk = 1 + (bi - 1) % 7
                    bi += 1
                    R.op("pe", lambda h, v=v, c0=c0, c1=c1, bnk=bnk: h.matmul(ps[0:4, bnk, 0:c1 - c0], lhsT=relb[:, :], rhs=OHG[:, v, c0:c1], start=True, stop=True),
                         reads=[B["relb"], B["OHG"]], writes=[B["psb%d" % bnk]])
                    R.op("dve", lambda h, v=v, c0=c0, c1=c1, bnk=bnk: h.tensor_copy(out=Gsb[:, v, c0:c1], in_=ps[0:4, bnk, 0:c1 - c0]),
                         reads=[B["psb%d" % bnk]], writes=[B["Gsb"]])
            dma(Gd.ap().rearrange("v h t -> h v t"), Gsb[:, :, :], "c13", reads=[B["Gsb"]], writes=[B["Gd"]])
            for hh in range(4):
                dma(Hsb4[:, hh, 0:1152], bass.AP(Gd, (0 * 4 + hh) * 1280, [[1, 128], [1, 1152]]), "c14_%d" % hh, reads=[B["Gd"]], writes=[B["Hsb%d" % hh]])
                dma(Hsb4[:, hh, 1152:1664], bass.AP(Gd, (1 * 4 + hh) * 1280, [[1, 128], [1, 512]]), "c14_%d" % hh, reads=[B["Gd"]], writes=[B["Hsb%d" % hh]])
                dma(Hsb4[:, hh, 1664:2176], bass.AP(Gd, (2 * 4 + hh) * 1280 + 640, [[1, 128], [1, 512]]), "c14_%d" % hh, reads=[B["Gd"]], writes=[B["Hsb%d" % hh]])
            for hh in range(4):
                bA, bB = 1 + 2 * (hh % 2), 2 + 2 * (hh % 2)
                R.op("pe", lambda h, hh=hh, bA=bA: h.matmul(ps[:, bA, 0:384], lhsT=Jf[:, :], rhs=Hsb4[:, hh, 384:768], start=True, stop=True),
                     reads=[B["Jf"], B["Hsb%d" % hh]], writes=[B["psb%d" % bA]])
                R.op("pe", lambda h, hh=hh, bA=bA: h.matmul(ps[:, bA, 384:512], lhsT=Jf[:, :], rhs=Hsb4[:, hh, 1152 + 384:1152 + 512], start=True, stop=True),
                     reads=[B["Jf"], B["Hsb%d" % hh]], writes=[B["psb%d" % bA]])
                R.op("pe", lambda h, hh=hh, bB=bB: h.matmul(ps[:, bB, 0:128], lhsT=Jf[:, :], rhs=Hsb4[:, hh, 1664:1792], start=True, stop=True),
                     reads=[B["Jf"], B["Hsb%d" % hh]], writes=[B["psb%d" % bB]])
                for (dst0, n, bank, src0, col) in [(0, 384, bA, 0, 64), (384, 384, bA, 0, 65), (768, 128, bA, 384, 16), (896, 128, bB, 0, 63)]:
                    R.op("dve", lambda h, hh=hh, dst0=dst0, n=n, bank=bank, src0=src0, col=col: h.tensor_scalar(
                        out=BTw[:, hh, dst0:dst0 + n], in0=ps[:, bank, src0:src0 + n], scalar1=cb8[:, hh, col:col + 1], scalar2=None, op0=ALU.subtract),
                         reads=[B["psb%d" % bank], B["cb8"]], writes=[B["BTw"]])
            for g in range(4):
                for q in range(2):
                    R.op("pe", lambda h, g=g, q=q: h.matmul(ps[:, 6, q * 128:(q + 1) * 128], lhsT=CS[:, q * 128:(q + 1) * 128], rhs=wfs[:, g, :], start=True, stop=True),
                         reads=[B["CS"], B["wfs"]], writes=[B["psb6"]])
                R.op("dve", lambda h, g=g: h.tensor_copy(out=CSW[:, g, :], in_=ps[:, 6, 0:256]), reads=[B["psb6"]], writes=[B["CSW"]])
            R.emit()
            B.clear()
            if dbg:
                dma(dcb.ap()[:, :], cb[:, :, :].rearrange("p a b -> p (a b)"), "g0")
                dma(dbt.ap()[:, :], BTw[:, :, :].rearrange("p a b -> p (a b)"), "g1")
                dma(dcsw.ap()[:, :], CSW[:, :, :].rearrange("p a b -> p (a b)"), "g2")
                dma(dlam.ap()[:, :], lamc[:, :], "g3")
                R.emit()
                if dbg == "S":
                    return nc

        def load_weight(wst, dst, src_ap, nch, ncols, name, cw=1024):
            for c in range(nch):
                dma(dst[:, c, 0:ncols], src_ap[c * 128:(c + 1) * 128, 0:ncols], "%s_%d" % (name, c), eng="pool", writes=[B[name + "_%d" % c]])

        def rstd_ops(ss_ap, out_ap, scale, eps_ap, tag, bufs_r, buf_w, tmp_ap):
            R.op("act", lambda h: h.activation(out=tmp_ap, in_=ss_ap, func=AF.Ln, bias=eps_ap, scale=scale),
                 reads=bufs_r + [B["epsc"]], writes=[B[tag + "_t"]])
            R.op("act", lambda h: h.activation(out=out_ap, in_=tmp_ap, func=AF.Exp, scale=-0.5),
                 reads=[B[tag + "_t"]], writes=[buf_w])

        for ji, job in enumerate(JOBS[:njobs]):
            NKV, NQ, N2, RR, NK2 = job["NKV"], job["NQ"], job["N2"], job["R"], job["NK2"]
            NB = NKV // 512
            NBQ = NQ // 512
            x_ap = xj[ji].ap()
            with contextlib.ExitStack() as A:
                Wb = sbt(A, "Wb", [128, 8, 3072], BF16)
                load_weight(None, Wb, win.ap(), 8, 3072, "Wb")
                xt = sbt(A, "xt", [128, 4, D], F32)
                junk = sbt(A, "junk", [128, D], BF16)
                xnb = sbt(A, "xnb", [128, 4, D], BF16)
                xnT = sbt(A, "xnT", [128, 2, 8, 512], BF16)
                ssA = sbt(A, "ssA", [128, 3, 3, 4], F32)
                FMst = sbt(A, "FMst", [128, 2, 12, 512], BF16)
                TMst = sbt(A, "TMst", [128, 2, 4, 4, 129], BF16)
                G2st = sbt(A, "G2st", [128, 2, 4, 4, 128], BF16)
                GFst = sbt(A, "GFst", [128, 2, 4, 4, 128], BF16)
                g2f = sbt(A, "g2f", [128, 2, 512], F32)
                for s in range(2):
                    R.op("pool", lambda h, s=s: h.memset(TMst[:, s, :, :, 128:129], 1.0), writes=[B["TMst%d" % s]])

                def FE1_items(blk):
                    b3 = blk % 3
                    items = []
                    items.append(lambda: R.op("dve", lambda h, b3=b3: h.memset(ssA[:, b3, 0, :], 0.0), writes=[B["ss%d" % b3]]))
                    for ti in range(4):
                        items.append(lambda ti=ti: R.op("act", lambda h, b3=b3, ti=ti: h.activation(out=junk[:, :], in_=xt[:, ti, :], func=AF.Square,
                                                                                            accum_out=ssA[:, b3, 0, ti:ti + 1]),
                                                        reads=[B["xt%d" % ti]], writes=[B["junk"], B["ss%d" % b3]]))
                    items.append(lambda: rstd_ops(ssA[:, b3, 0, :], ssA[:, b3, 2, :], 1.0 / D, epsc[:, 0:1], "rA%d" % b3, [B["ss%d" % b3]], B["rs%d" % b3], ssA[:, b3, 1, :]))
                    for ti in range(4):
                        items.append(lambda ti=ti: R.op("dve", lambda h, ti=ti, b3=b3: h.scalar_tensor_tensor(out=xnb[:, ti, :], in0=xt[:, ti, :], scalar=ssA[:, b3, 2, ti:ti + 1],
                                                                                                   in1=gvec[:, 0, :], op0=ALU.mult, op1=ALU.mult),
                                                        reads=[B["xt%d" % ti], B["rs%d" % b3]], writes=[B["xnb%d" % ti]]))
                    return items

                def FE1(blk):
                    for it in FE1_items(blk):
                        it()

                def FE2_tile(blk, ti):
                    bs = blk % 2
                    tb = (0, 7)[ti % 2]
                    for c in range(8):
                        R.op("pe", lambda h, ti=ti, c=c, tb=tb: h.transpose(out=psb(tb)[:, c * 128:(c + 1) * 128], in_=xnb[:, ti, c * 128:(c + 1) * 128], identity=ident[:, :]),
                             reads=[B["xnb%d" % ti]], writes=[B["pb%d" % tb]])
                    if ti % 2 == 0:
                        R.op("act", lambda h, bs=bs, ti=ti, tb=tb: h.activation(out=xnT[:, bs, :, ti * 128:(ti + 1) * 128],
                                                                                in_=psb(tb).rearrange("p (c t) -> p c t", c=8), func=AF.Copy),
                             reads=[B["pb%d" % tb]], writes=[B["xnT%d" % bs]])
                    else:
                        R.op("dve", lambda h, bs=bs, ti=ti, tb=tb: h.tensor_copy(out=xnT[:, bs, :, ti * 128:(ti + 1) * 128],
                                                                                 in_=psb(tb).rearrange("p (c t) -> p c t", c=8)),
                             reads=[B["pb%d" % tb]], writes=[B["xnT%d" % bs]])

                def FE2(blk):
                    for ti in range(4):
                        FE2_tile(blk, ti)

                def loads(blk):
                    for ti in range(4):
                        t = blk * 4 + ti
                        dma(xt[:, ti, :], x_ap[t * 128:(t + 1) * 128, :], "x%d" % ti, writes=[B["xt%d" % ti]])

                fmk = [0]

                def TM_tile(blk, ti):
                    mine = blk < NBQ
                    bs = blk % 2
                    vb = 3 + 3 * (ti % 2)
                    for c in range(8):
                        R.op("pe", lambda h, bs=bs, ti=ti, c=c, vb=vb: h.matmul(ps[:, vb, :], lhsT=xnT[:, bs, c, ti * 128:(ti + 1) * 128], rhs=Wb[:, c, 1024:1536],
                                                                                 start=(c == 0), stop=(c == 7)),
                             reads=[B["xnT%d" % bs], B["Wb_%d" % c]], writes=[B["pb%d" % vb]])
                    R.op("dve", lambda h, bs=bs, ti=ti, vb=vb: h.tensor_copy(out=TMst[:, bs, :, ti, 0:128], in_=ps[:, vb, :].rearrange("p (a b) -> p a b", a=4)),
                         reads=[B["pb%d" % vb]], writes=[B["TMst%d" % bs]])
                    if mine:
                        gb = (4, 1)[ti % 2]
                        fbk = (5, 2)[ti % 2]
                        gs = ti % 2
                        for c in range(8):
                            R.op("pe", lambda h, bs=bs, ti=ti, c=c, gb=gb: h.matmul(ps[:, gb, :], lhsT=xnT[:, bs, c, ti * 128:(ti + 1) * 128],
                                                                                  rhs=Wb[:, c, 1536:2048], start=(c == 0), stop=(c == 7)),
                                 reads=[B["xnT%d" % bs], B["Wb_%d" % c]], writes=[B["pb%d" % gb]])
                        R.op("act", lambda h, gb=gb, gs=gs: h.activation(out=g2f[:, gs, :], in_=ps[:, gb, :], func=AF.Silu),
                             reads=[B["pb%d" % gb]], writes=[B["g2f%d" % gs]])
                        R.op("dve", lambda h, bs=bs, ti=ti, gs=gs: h.tensor_tensor(out=G2st[:, bs, :, ti, :], in0=g2f[:, gs, :].rearrange("p (a b) -> p a b", a=4),
                                                                                  in1=gsub2[:, :].unsqueeze(1).to_broadcast([128, 4, 128]), op=ALU.mult),
                             reads=[B["g2f%d" % gs]], writes=[B["G2st%d" % bs]])
                        for c in range(8):
                            R.op("pe", lambda h, bs=bs, ti=ti, c=c, fbk=fbk: h.matmul(ps[:, fbk, :], lhsT=xnT[:, bs, c, ti * 128:(ti + 1) * 128],
                                                                           rhs=Wb[:, c, 2560:3072], start=(c == 0), stop=(c == 7)),
                                 reads=[B["xnT%d" % bs], B["Wb_%d" % c]], writes=[B["pb%d" % fbk]])
                        R.op("act", lambda h, bs=bs, ti=ti, fbk=fbk: h.activation(out=GFst[:, bs, :, ti, :], in_=ps[:, fbk, :].rearrange("p (a b) -> p a b", a=4), func=AF.Silu),
                             reads=[B["pb%d" % fbk]], writes=[B["GFst%d" % bs]])

                def FM(blk, items):
                    mine = blk < NBQ
                    bs = blk % 2
                    chunks = [(512 + hh * 128, hh) for hh in range(4)] + [(2048 + g * 128, 4 + g) for g in range(4)]
                    if mine:
                        chunks += [(hh * 128, 8 + hh) for hh in range(4)]
                    items = list(items)
                    for ci, (c0, idx) in enumerate(chunks):
                        fb = 1 + fmk[0] % 2
                        for c in range(8):
                            R.op("pe", lambda h, bs=bs, c=c, c0=c0, fb=fb: h.matmul(ps[:, fb, :], lhsT=Wb[:, c, c0:c0 + 128], rhs=xnT[:, bs, c, :], start=(c == 0), stop=(c == 7)),
                                 reads=[B["xnT%d" % bs], B["Wb_%d" % c]], writes=[B["pb%d" % fb]])
                        if fmk[0] % 2 == 0:
                            R.op("act", lambda h, bs=bs, idx=idx, fb=fb: h.activation(out=FMst[:, bs, idx, :], in_=ps[:, fb, :], func=AF.Copy),
                                 reads=[B["pb%d" % fb]], writes=[B["FMst%d" % bs]])
                        else:
                            R.op("dve", lambda h, bs=bs, idx=idx, fb=fb: h.tensor_copy(out=FMst[:, bs, idx, :], in_=ps[:, fb, :]),
                                 reads=[B["pb%d" % fb]], writes=[B["FMst%d" % bs]])
                        fmk[0] += 1
                        rem_chunks = len(chunks) - ci
                        ntake = -(-len(items) // rem_chunks)
                        for _ in range(ntake):
                            items.pop(0)()
                    tsl = slice(blk * 512, (blk + 1) * 512)
                    dma(KTd.ap()[:, :, tsl].rearrange("h p t -> p h t"), FMst[:, bs, 0:4, :], "oK%d" % bs, eng="pool", reads=[B["FMst%d" % bs]])
                    dma(UTd.ap()[:, :, tsl].rearrange("h p t -> p h t"), FMst[:, bs, 4:8, :], "oU%d" % bs, eng="pool", reads=[B["FMst%d" % bs]])
                    dma(Vd.ap()[:, :, blk * 4:(blk + 1) * 4, :].rearrange("h p t e -> p h t e"), TMst[:, bs, :, :, :], "oV%d" % bs, eng="pool", reads=[B["TMst%d" % bs]])
                    if mine:
                        dma(QTd.ap()[:, :, tsl].rearrange("h p t -> p h t"), FMst[:, bs, 8:12, :], "oQ%d" % bs, eng="pool", reads=[B["FMst%d" % bs]])
                        dma(G2d.ap()[:, :, blk * 4:(blk + 1) * 4, :].rearrange("h p t e -> p h t e"), G2st[:, bs, :, :, :], "oG%d" % bs, eng="pool", reads=[B["G2st%d" % bs]])
                        dma(GFd.ap()[:, :, blk * 4:(blk + 1) * 4, :].rearrange("h p t e -> p h t e"), GFst[:, bs, :, :, :], "oF%d" % bs, eng="pool", reads=[B["GFst%d" % bs]])

                loads(0)
                FE1(0)
                loads(1)
                FE2(0)
                FE1(1)
                loads(2)
                for blk in range(NB):
                    for ti in range(4):
                        if blk + 1 < NB:
                            FE2_tile(blk + 1, ti)
                        TM_tile(blk, ti)
                    items = FE1_items(blk + 2) if blk + 2 < NB else []
                    FM(blk, items)
                    if blk + 3 < NB:
                        loads(blk + 3)
                R.emit()
                B.clear()
            if dbg == "A":
                return nc

            with contextlib.ExitStack() as M:
                mixT = sbt(M, "mixT", [128, 8, NQ], BF16)
                with contextlib.ExitStack() as P:
                    KT2 = sbt(P, "KT", [128, 2, NKV], BF16)
                    QT2 = sbt(P, "QT", [128, 2, NQ], BF16)
                    Vh2 = sbt(P, "Vh", [128, 2, NKV // 128, 129], BF16)
                    G2h2 = sbt(P, "G2h", [128, 2, NQ // 128, 128], BF16)
                    PT = sbt(P, "PT", [128, 3, 2, 512], BF16)
                    stg = sbt(P, "stg", [128, 8, 129], F32)
                    r8 = sbt(P, "r8", [128, 8], F32)
                    o4 = sbt(P, "o4", [128, 4, 128], F32)
                    sq4 = sbt(P, "sq4", [128, 4, 128], F32)
                    ss4 = sbt(P, "ss4", [128, 3, 4], F32)
                    attb = sbt(P, "attb", [128, 4, 128], BF16)
                    NJ = NKV // 128
                    NM = NQ // 512

                    def near_info(m, j):
                        if ji == 0 or j <= 15:
                            tau = j - 4 * m
                            if -1 <= tau <= 4:
                                i0, i1 = max(0, tau - 1), min(3, tau + 1)
                                base = 0 if tau <= 1 else 384
                                return (64 if tau <= 1 else 65), (base + (1 - tau + i0) * 128, i0, i1 - i0 + 1)
                            return (64 if tau < -1 else 65), None
                        if j == 16 and m == 3:
                            return 16, (768, 3, 1)
                        if j == 63 and m == 0:
                            return 63, (896, 0, 1)
                        return j, None

                    step_id = 0
                    pend = [None]
                    def loadsB(hh):
                        hp = hh % 2
                        dma(QT2[:, hp, 0:NQ // 2], QTd.ap()[hh, :, 0:NQ // 2], "lQ%d_0" % hp, writes=[B["QT%d_0" % hp]])
                        for ci in range(4):
                            k0, k1 = ci * (NKV // 4), (ci + 1) * (NKV // 4)
                            dma(KT2[:, hp, k0:k1], KTd.ap()[hh, :, k0:k1], "lK%d_%d" % (hp, ci), writes=[B["KT%d_%d" % (hp, ci)]])
                            dma(Vh2[:, hp, k0 // 128:k1 // 128, :], Vd.ap()[hh, :, k0 // 128:k1 // 128, :], "lV%d_%d" % (hp, ci), writes=[B["Vh%d_%d" % (hp, ci)]])
                        dma(QT2[:, hp, NQ // 2:NQ], QTd.ap()[hh, :, NQ // 2:NQ], "lQ%d_1" % hp, writes=[B["QT%d_1" % hp]])
                        dma(G2h2[:, hp, :, :], G2d.ap()[hh, :, 0:NQ // 128, :], "lG%d" % hp, writes=[B["G2h%d" % hp]])

                    loadsB(0)
                    for hh in range(4):
                        hp = hh % 2
                        if pend[0] is not None:
                            for it in pend[0]:
                                it[0]()
                            pend[0] = None
                        if hh + 1 < 4:
                            loadsB(hh + 1)
                        KT = KT2[:, hp, :]
                        QT = QT2[:, hp, :]
                        Vh = Vh2[:, hp, :, :]
                        G2h = G2h2[:, hp, :, :]
                        bG2h = B["G2h%d" % hp]
                        JQ = NJ // 4
                        steps = [(m, j) for m in range(NM) for j in range(NJ)]

                        def QK(si, m, j):
                            slot = si % 2
                            col, off = near_info(m, j)
                            for c in range(2):
                                R.op("pe", lambda h, c=c, j=j, m=m, slot=slot, off=off, KT=KT, QT=QT: h.matmul(ps[:, slot * 2 + c, :], lhsT=KT[64 * c:64 * c + 64, j * 128:(j + 1) * 128],
                                                                                                 rhs=QT[64 * c:64 * c + 64, m * 512:(m + 1) * 512], start=True, stop=(off is None)),
                                     reads=[B["KT%d_%d" % (hp, j // JQ)], B["QT%d_%d" % (hp, (2 * m) // NM)]], writes=[B["S%d" % slot]])
                            if off is not None:
                                for c in range(2):
                                    R.op("pe", lambda h, c=c, slot=slot, off=off, hh=hh: h.matmul(ps[:, slot * 2 + c, off[1] * 128:(off[1] + off[2]) * 128], lhsT=ident[:, :],
                                                                                                  rhs=BTw[:, hh, off[0]:off[0] + off[2] * 128], start=False, stop=True),
                                         reads=[], writes=[B["S%d" % slot]])
                            return col

                        cols = {}
                        cols[0] = QK(step_id, *steps[0])
                        for k, (m, j) in enumerate(steps):
                            si = step_id + k
                            if k + 1 < len(steps):
                                cols[k + 1] = QK(si + 1, *steps[k + 1])
                            slot = si % 2
                            p3 = si % 3
                            col = cols.pop(k)
                            R.op("act", lambda h, slot=slot, p3=p3, col=col, hh=hh: h.activation(out=PT[:, p3, :, :], in_=ps[:, slot * 2:slot * 2 + 2, :], func=AF.Exp,
                                                                                                 bias=cb[:, hh, col:col + 1], scale=0.125),
                                 reads=[B["S%d" % slot]], writes=[B["PT%d" % p3]])
                            for c in range(2):
                                for i in range(4):
                                    idx = c * 4 + i
                                    R.op("pe", lambda h, p3=p3, c=c, i=i, idx=idx, j=j, Vh=Vh: h.matmul(ps[:, 4 + idx // 3, (idx % 3) * 129:(idx % 3) * 129 + 129],
                                                                                               lhsT=PT[:, p3, c, i * 128:(i + 1) * 128], rhs=Vh[:, j, :],
                                                                                               start=(j == 0 and idx % 3 == 0), stop=(j == NJ - 1),
                                                                                               skip_group_check=True),
                                         reads=[B["PT%d" % p3], B["Vh%d_%d" % (hp, j // JQ)]], writes=[B["acc%d" % (4 + idx // 3)]])
                            if j == NJ - 1:
                                for bk, (a0, n) in enumerate([(0, 3), (3, 3), (6, 2)]):
                                    R.op("dve", lambda h, bk=bk, a0=a0, n=n: h.tensor_copy(out=stg[:, a0:a0 + n, :],
                                                                                          in_=ps[:, 4 + bk, 0:n * 129].rearrange("p (a b) -> p a b", b=129)),
                                         reads=[B["acc%d" % (4 + bk)]], writes=[B["stg%d" % bk]])
                                R.op("dve", lambda h: h.reciprocal(out=r8[:, :], in_=stg[:, :, 128]), reads=[B["stg0"], B["stg1"], B["stg2"]], writes=[B["r8"]])
                                R.op("dve", lambda h: h.tensor_scalar(out=r8[:, 4:8], in0=r8[:, 4:8], scalar1=lamc[:, 3:4], scalar2=None, op0=ALU.mult),
                                     reads=[B["r8"]], writes=[B["r8"]])
                                R.op("dve", lambda h: h.tensor_tensor(out=stg[:, :, 0:128], in0=stg[:, :, 0:128], in1=r8[:, :].unsqueeze(2).to_broadcast([128, 8, 128]), op=ALU.mult),
                                     reads=[B["r8"], B["stg0"], B["stg1"], B["stg2"]], writes=[B["stg0"], B["stg1"], B["stg2"]])
                                R.op("dve", lambda h: h.tensor_tensor(out=o4[:, :, :], in0=stg[:, 0:4, 0:128], in1=stg[:, 4:8, 0:128], op=ALU.add),
                                     reads=[B["stg0"], B["stg1"], B["stg2"]], writes=[B["o4"]])
                                R.op("dve", lambda h: h.tensor_tensor(out=sq4[:, :, :], in0=o4[:, :, :], in1=o4[:, :, :], op=ALU.mult),
                                     reads=[B["o4"]], writes=[B["sq4"]])
                                R.op("dve", lambda h: h.tensor_reduce(out=ss4[:, 0, :], in_=sq4[:, :, :], axis=AX.X, op=ALU.add),
                                     reads=[B["sq4"]], writes=[B["ss4"]])
                                def _tail1(m=m, G2h=G2h, bG2h=bG2h, hh=hh):
                                    rstd_ops(ss4[:, 0, :], ss4[:, 2, :], 1.0 / 128, epsc[:, 1:2], "rB", [B["ss4"]], B["rs4"], ss4[:, 1, :])
                                    R.op("dve", lambda h: h.tensor_tensor(out=o4[:, :, :], in0=o4[:, :, :], in1=ss4[:, 2, :].unsqueeze(2).to_broadcast([128, 4, 128]), op=ALU.mult),
                                         reads=[B["rs4"], B["o4"]], writes=[B["o4"]])
                                    R.op("dve", lambda h, m=m, G2h=G2h: h.tensor_tensor(out=attb[:, :, :], in0=o4[:, :, :], in1=G2h[:, m * 4:(m + 1) * 4, :], op=ALU.mult),
                                         reads=[B["o4"], bG2h], writes=[B["attb"]])

                                def _tail2(m=m, hh=hh):
                                    for i in range(4):
                                        R.op("pe", lambda h, i=i: h.transpose(out=psb(7)[:, i * 128:(i + 1) * 128], in_=attb[:, i, :], identity=ident[:, :]),
                                             reads=[B["attb"]], writes=[B["pb7"]])
                                    R.op("dve", lambda h, hh=hh, m=m: h.tensor_copy(out=mixT[:, hh, m * 512:(m + 1) * 512], in_=psb(7)[:, 0:512]),
                                         reads=[B["pb7"]], writes=[B["mixT"]])
                                pend[0] = [[_tail1, 5], [_tail2, 12]]
                            elif pend[0] is not None:
                                for it in pend[0]:
                                    it[1] -= 1
                                while pend[0] and pend[0][0][1] <= 0:
                                    pend[0].pop(0)[0]()
                                if not pend[0]:
                                    pend[0] = None
                        step_id += len(steps)
                    if pend[0] is not None:
                        for it in pend[0]:
                            it[0]()
                        pend[0] = None
                    R.emit()
                    B.clear()
                if dbg == "B":
                    dma(dmix.ap()[:, 0:8 * NQ], mixT[:, :, :].rearrange("p a b -> p (a b)"), "g4")
                    R.emit()
                    return nc

                with contextlib.ExitStack() as P:
                    uT = sbt(P, "uT", [128, 2, NKV], BF16)
                    GFg = sbt(P, "GFg", [128, 2, NQ // 128, 128], BF16)
                    PQ = sbt(P, "PQ", [128, 2, 128, N2], BF16)
                    mt = sbt(P, "mt", [128, 2, 2, 4, 2, 128], BF16)
                    X2 = sbt(P, "X2", [128, 2, 4, 2, 128], BF16)
                    Ab = sbt(P, "Ab", [128, 2, 4, 2, 128], BF16)
                    fng = sbt(P, "fng", [128, NK2, 128], BF16)
                    NU = 128 // RR
                    CPB = 512 // NK2
                    def loadsC(g):
                        dma(uT[:, g % 2, :], UTd.ap()[g, :, 0:NKV], "lU%d" % (g % 2), writes=[B["uT%d" % (g % 2)]])
                        dma(GFg[:, g % 2, :, :], GFd.ap()[g, :, 0:NQ // 128, :], "lF%d" % (g % 2), writes=[B["GFg%d" % (g % 2)]])

                    loadsC(0)
                    for g in range(4):
                        gp2 = g % 2
                        if g + 1 < 4:
                            loadsC(g + 1)
                        for s2 in range(N2):
                            bnk = (0, 1, 6, 7)[(s2 // 2) % 4]
                            R.op("pe", lambda h, s2=s2, bnk=bnk, g=g: h.matmul(ps[:, bnk, (s2 % 2) * 256:(s2 % 2) * 256 + 256], lhsT=uT[:, g % 2, s2:NKV:N2], rhs=CSW[:, g, :],
                                                                              start=True, stop=True),
                                 reads=[B["uT%d" % (g % 2)]], writes=[B["pb%d" % bnk]])
                            if s2 % 2 == 1:
                                eng = ("act", "dve")[(s2 // 2) % 2]
                                src = ps[:, bnk, :].rearrange("p (s q d) -> p q d s", s=2, q=2)
                                dst = PQ[:, :, :, s2 - 1:s2 + 1]
                                if eng == "act":
                                    R.op("act", lambda h, src=src, dst=dst: h.activation(out=dst, in_=src, func=AF.Copy), reads=[B["pb%d" % bnk]], writes=[B["PQ"]])
                                else:
                                    R.op("dve", lambda h, src=src, dst=dst: h.tensor_copy(out=dst, in_=src), reads=[B["pb%d" % bnk]], writes=[B["PQ"]])
                        NXS = 3 if RR == 2 else 2
                        XB0 = (2, 4, 0)
                        xbufs_of = lambda xs: [B["pb0"], B["pb1"]] if xs == 2 else [B["X%d" % xs]]

                        def s1(ub):
                            sl = ub % NXS
                            b0 = XB0[sl]
                            xbufs = xbufs_of(sl)
                            for ui in range(4):
                                u = ub * 4 + ui
                                dstp = ps[:, b0 + ui // 2, (ui % 2) * 256:(ui % 2) * 256 + 256]
                                R.op("pe", lambda h, u=u, dstp=dstp: h.matmul(dstp, lhsT=PQ[:, 0, u * RR:(u + 1) * RR, :].rearrange("p d s -> p (d s)"), rhs=F1ab[:, 0:256],
                                                                              start=True, stop=False),
                                     reads=[B["PQ"]], writes=xbufs)
                                R.op("pe", lambda h, u=u, dstp=dstp: h.matmul(dstp, lhsT=PQ[:, 1, u * RR:(u + 1) * RR, :].rearrange("p d s -> p (d s)"), rhs=F1ab[:, 256:512],
                                                                              start=False, stop=True),
                                     reads=[B["PQ"]], writes=xbufs)
                        def s2(ub):
                            xs = ub % NXS
                            b0 = XB0[xs]
                            sl = ub % 2
                            X = ps[:, b0:b0 + 2, :].rearrange("p b (u a k) -> p (b u) a k", u=2, a=2)
                            R.op("act", lambda h, sl=sl, X=X: h.activation(out=X2[:, sl, :, :, :], in_=X, func=AF.Copy), reads=xbufs_of(xs), writes=[B["X2%d" % sl]])
                            for mi in range(2):
                                R.op("dve", lambda h, sl=sl, mi=mi: h.tensor_tensor(out=mt[:, sl, mi, :, :, :], in0=X2[:, sl, :, :, :],
                                                                                   in1=twb[ji][:, mi, :, :].unsqueeze(1).to_broadcast([128, 4, 2, 128]), op=ALU.mult),
                                     reads=[B["X2%d" % sl]], writes=[B["mt%d%d" % (sl, mi)]])
                            for bi in range(2):
                                R.op("dve", lambda h, sl=sl, bi=bi: h.tensor_tensor(out=Ab[:, sl, :, bi, :], in0=mt[:, sl, bi, :, 0, :], in1=mt[:, sl, bi, :, 1, :], op=ALU.add),
                                     reads=[B["mt%d%d" % (sl, bi)]], writes=[B["Ab%d" % sl]])
                            YB = [6, 7, 0, 1]
                            UPB = 512 // NK2
                            for ui in range(4):
                                u = ub * 4 + ui
                                ycol = (u % UPB) * NK2
                                for r in range(RR):
                                    yb = YB[r]
                                    for bi in range(2):
                                        R.op("pe", lambda h, sl=sl, ui=ui, r=r, bi=bi, yb=yb, ycol=ycol: h.matmul(
                                            ps[:, yb, ycol:ycol + NK2], lhsT=Ab[r * N2:(r + 1) * N2, sl, ui, bi, :], rhs=F2[ji][r * N2:(r + 1) * N2, bi, 0:NK2],
                                            start=(bi == 0), stop=(bi == 1), tile_position=(r * N2, 0), skip_group_check=True),
                                             reads=[B["Ab%d" % sl]], writes=[B["pb%d" % yb]])
                                if u % UPB == UPB - 1:
                                    u0 = u - (UPB - 1)
                                    for r in range(RR):
                                        yb = YB[r]
                                        csl = slice(u0 * RR + r, (u0 + UPB) * RR, RR)
                                        R.op("dve", lambda h, yb=yb, csl=csl, gp2=gp2: h.tensor_tensor(out=fng[:, :, csl],
                                                                                            in0=ps[:, yb, :].rearrange("p (c k) -> p k c", k=NK2),
                                                                                            in1=GFg[:, gp2, 0:NK2, csl], op=ALU.mult),
                                             reads=[B["pb%d" % yb], B["GFg%d" % gp2]], writes=[B["fng"]])
                        LA = NXS - 1
                        for ub in range(min(LA, NU // 4)):
                            s1(ub)
                        for ub in range(NU // 4):
                            if ub + LA < NU // 4:
                                s1(ub + LA)
                            s2(ub)
                        for t4 in range(NK2 // 4):
                            tb = (0, 1)[t4 % 2]
                            for i in range(4):
                                R.op("pe", lambda h, t4=t4, i=i, tb=tb: h.transpose(out=psb(tb)[:, i * 128:(i + 1) * 128], in_=fng[:, t4 * 4 + i, :], identity=ident[:, :]),
                                     reads=[B["fng"]], writes=[B["pb%d" % tb]])
                            if t4 % 2 == 0:
                                R.op("act", lambda h, g=g, t4=t4, tb=tb: h.activation(out=mixT[:, 4 + g, t4 * 512:(t4 + 1) * 512], in_=psb(tb)[:, 0:512], func=AF.Copy),
                                     reads=[B["pb%d" % tb]], writes=[B["mixT"]])
                            else:
                                R.op("dve", lambda h, g=g, t4=t4, tb=tb: h.tensor_copy(out=mixT[:, 4 + g, t4 * 512:(t4 + 1) * 512], in_=psb(tb)[:, 0:512]),
                                     reads=[B["pb%d" % tb]], writes=[B["mixT"]])
                    R.emit()
                    B.clear()
                if dbg == "C":
                    dma(dmix.ap()[:, 0:8 * NQ], mixT[:, :, :].rearrange("p a b -> p (a b)"), "g4")
                    R.emit()
                    return nc

                with contextlib.ExitStack() as P:
                    wo_b = sbt(P, "wo_b", [128, 8, D], BF16)
                    wg_b = sbt(P, "wg_b", [128, 8, D], BF16)
                    wp_b = sbt(P, "wp_b", [128, 2, D], BF16)
                    load_weight(None, wo_b, wout.ap(), 8, D, "wo")
                    load_weight(None, wg_b, wgate.ap(), 8, D, "wg")
                    load_weight(None, wp_b, wproj.ap(), 2, D, "wp")
                    xt = sbt(P, "xtD", [128, 2, D], F32)
                    ptl = sbt(P, "ptl", [128, 2, 256], F32)
                    pbf = sbt(P, "pbf", [128, 2, 256], BF16)
                    pT = sbt(P, "pT", [128, 2, 2, 128], BF16)
                    h1 = sbt(P, "h1", [128, 2, D], F32)
                    junk = sbt(P, "junkD", [128, D], BF16)
                    ssD = sbt(P, "ssD", [128, 2, 6], F32)
                    hnb = sbt(P, "hnb", [128, 2, D], BF16)
                    hnT = sbt(P, "hnT", [128, 2, 8, 128], BF16)
                    sg = sbt(P, "sg", [128, 2, D], F32)
                    gp = sbt(P, "gp", [128, 2, D], F32)
                    NT = NQ // 128

                    def stA1(t):
                        s = t % 2
                        dma(xt[:, s, :], x_ap[t * 128:(t + 1) * 128, :], "dx%d" % s, writes=[B["xt%d" % s]])
                        dma(ptl[:, s, :], pj[ji].ap()[t * 128:(t + 1) * 128, :], "dp%d" % s, writes=[B["ptl%d" % s]])
                        R.op("dve", lambda h, s=s: h.memset(ssD[:, s, 0:1], 0.0), writes=[B["ssa%d" % s]])
                        R.op("dve", lambda h, s=s: h.memset(ssD[:, s, 3:4], 0.0), writes=[B["ssb%d" % s]])
                        for half in range(2):
                            for c in range(8):
                                R.op("pe", lambda h, t=t, c=c, half=half: h.matmul(ps[:, half, :], lhsT=mixT[:, c, t * 128:(t + 1) * 128], rhs=wo_b[:, c, half * 512:(half + 1) * 512],
                                                                                   start=(c == 0), stop=(c == 7)),
                                     reads=[B["wo_%d" % c]], writes=[B["pb01"]])
                        R.op("dve", lambda h, s=s: h.tensor_tensor(out=h1[:, s, :], in0=ps[:, 0:2, :].rearrange("p a b -> p (a b)"), in1=xt[:, s, :], op=ALU.add),
                             reads=[B["pb01"], B["xt%d" % s]], writes=[B["h1%d" % s]])
                        R.op("act", lambda h, s=s: h.activation(out=junk[:, :], in_=h1[:, s, :], func=AF.Square, accum_out=ssD[:, s, 0:1]),
                             reads=[B["h1%d" % s]], writes=[B["junk"], B["ssa%d" % s]])
                        rstd_ops(ssD[:, s, 0:1], ssD[:, s, 2:3], 1.0 / D, epsc[:, 0:1], "rD%d" % s, [B["ssa%d" % s]], B["rsa%d" % s], ssD[:, s, 1:2])
                        R.op("dve", lambda h, s=s: h.scalar_tensor_tensor(out=hnb[:, s, :], in0=h1[:, s, :], scalar=ssD[:, s, 2:3], in1=gvec[:, 1, :], op0=ALU.mult, op1=ALU.mult),
                             reads=[B["h1%d" % s], B["rsa%d" % s]], writes=[B["hnb%d" % s]])
                        R.op("dve", lambda h, s=s: h.tensor_copy(out=pbf[:, s, :], in_=ptl[:, s, :]), reads=[B["ptl%d" % s]], writes=[B["pbf%d" % s]])

                    def stA2(t):
                        s = t % 2
                        for c in range(8):
                            R.op("pe", lambda h, s=s, c=c: h.transpose(out=psb(2)[:, c * 128:(c + 1) * 128], in_=hnb[:, s, c * 128:(c + 1) * 128], identity=ident[:, :]),
                                 reads=[B["hnb%d" % s]], writes=[B["pb2"]])
                        R.op("act", lambda h, s=s: h.activation(out=hnT[:, s, :, :], in_=psb(2).rearrange("p (c t) -> p c t", c=8), func=AF.Copy),
                             reads=[B["pb2"]], writes=[B["hnT%d" % s]])
                        for c in range(2):
                            R.op("pe", lambda h, s=s, c=c: h.transpose(out=psb(5)[:, c * 128:(c + 1) * 128], in_=pbf[:, s, c * 128:(c + 1) * 128], identity=ident[:, :]),
                                 reads=[B["pbf%d" % s]], writes=[B["pb5"]])
                        R.op("dve", lambda h, s=s: h.tensor_copy(out=pT[:, s, :, :], in_=psb(5)[:, 0:256].rearrange("p (c t) -> p c t", c=2)),
                             reads=[B["pb5"]], writes=[B["pT%d" % s]])

                    def stB(t):
                        s = t % 2
                        for half in range(2):
                            for c in range(8):
                                R.op("pe", lambda h, s=s, c=c, half=half: h.matmul(ps[:, 3 + half, :], lhsT=hnT[:, s, c, :], rhs=wg_b[:, c, half * 512:(half + 1) * 512],
                                                                                   start=(c == 0), stop=(c == 7)),
                                     reads=[B["hnT%d" % s], B["wg_%d" % c]], writes=[B["pb34"]])
                        R.op("act", lambda h, s=s: h.activation(out=sg[:, s, :], in_=ps[:, 3:5, :].rearrange("p a b -> p (a b)"), func=AF.Sigmoid),
                             reads=[B["pb34"]], writes=[B["sg%d" % s]])
                        for half in range(2):
                            for c in range(2):
                                R.op("pe", lambda h, s=s, c=c, half=half: h.matmul(ps[:, 6 + half, :], lhsT=pT[:, s, c, :], rhs=wp_b[:, c, half * 512:(half + 1) * 512],
                                                                                   start=(c == 0), stop=(c == 1)),
                                     reads=[B["pT%d" % s], B["wp_%d" % c]], writes=[B["pb67"]])
                        R.op("dve", lambda h, s=s: h.tensor_tensor(out=gp[:, s, :], in0=ps[:, 6:8, :].rearrange("p a b -> p (a b)"), in1=sg[:, s, :], op=ALU.mult),
                             reads=[B["pb67"], B["sg%d" % s]], writes=[B["gp%d" % s]])
                        R.op("dve", lambda h, s=s: h.tensor_tensor(out=gp[:, s, :], in0=gp[:, s, :], in1=h1[:, s, :], op=ALU.add),
                             reads=[B["gp%d" % s], B["h1%d" % s]], writes=[B["gp%d" % s]])
                        R.op("act", lambda h, s=s: h.activation(out=junk[:, :], in_=gp[:, s, :], func=AF.Square, accum_out=ssD[:, s, 3:4]),
                             reads=[B["gp%d" % s]], writes=[B["junk"], B["ssb%d" % s]])
                        rstd_ops(ssD[:, s, 3:4], ssD[:, s, 5:6], 1.0 / D, epsc[:, 0:1], "rE%d" % s, [B["ssb%d" % s]], B["rsb%d" % s], ssD[:, s, 4:5])
                        R.op("dve", lambda h, s=s: h.scalar_tensor_tensor(out=sg[:, s, :], in0=gp[:, s, :], scalar=ssD[:, s, 5:6], in1=gvec[:, 2, :], op0=ALU.mult, op1=ALU.mult),
                             reads=[B["gp%d" % s], B["rsb%d" % s]], writes=[B["sg%d" % s]])
                        dma(yj[ji].ap()[t * 128:(t + 1) * 128, :], sg[:, s, :], "y%d" % s, eng="pool", reads=[B["sg%d" % s]])

                    stA1(0)
                    stA2(0)
                    for t in range(NT):
                        if t + 1 < NT:
                            stA1(t + 1)
                        stB(t)
                        if t + 1 < NT:
                            stA2(t + 1)
                    R.emit()
                    B.clear()
    return nc


def _rel_bucket_table():
    rel = np.arange(-639, 640, dtype=np.int32)
    ret = np.where(rel > 0, 16, 0)
    n = np.abs(rel)
    nf = np.maximum(n, 1).astype(np.float32)
    large = 8 + (np.log(nf / np.float32(8)) / np.float32(math.log(128 / 8)) * np.float32(8)).astype(np.int32)
    large = np.minimum(large, 15)
    return ret + np.where(n < 8, n, large)


def _host_consts():
    bf = ml_dtypes.bfloat16
    c = {}
    c["ident"] = np.eye(128, dtype=np.float32).astype(bf)
    c["Jm"] = np.ascontiguousarray(np.eye(128, dtype=np.float32)[::-1])
    i = np.arange(128)
    ang = 2 * np.pi * np.outer(i, i) / 128.0
    Cm, Sm = np.cos(ang), np.sin(ang)
    c["CS"] = (np.concatenate([Cm, Sm], 1) / np.sqrt(128.0)).astype(np.float32)
    c["F1ab"] = np.concatenate([Cm, Sm, -Sm, Cm], 1).astype(np.float32).astype(bf)
    bucket = _rel_bucket_table()
    per_core = []
    for core in range(NCORES):
        r = core % 4
        d = {}
        for ji, job in enumerate(JOBS):
            N = job["NKV"]; N2 = job["N2"]; RR = job["R"]; NK2 = job["NK2"]
            off = 0 if ji == 0 else 2048 * r
            s2 = np.arange(N2)[:, None].astype(np.float64)
            k1 = np.arange(128)[None, :].astype(np.float64)
            th = 2 * np.pi * (((s2 + off) * (k1 + off)) % N) / N
            twr, twi = np.cos(th), -np.sin(th)
            T = np.stack([twr, twi, -twr], 1)
            d["tw%d" % ji] = np.ascontiguousarray(np.tile(T, (RR, 1, 1))).astype(np.float32)
            k2 = np.arange(NK2)[None, :].astype(np.float64)
            ph = 2 * np.pi * ((s2 * k2) % N2) / N2
            F = np.zeros((N2, 2, 32), np.float64)
            F[:, 0, :NK2] = np.cos(ph) / np.sqrt(N)
            F[:, 1, :NK2] = np.sin(ph) / np.sqrt(N)
            d["F2%d" % ji] = np.ascontiguousarray(np.tile(F, (RR, 1, 1))).astype(np.float32).astype(bf)
        OHG = np.zeros((32, 3, 1280), np.float32)
        tp = np.arange(1279)
        rel = 639 - tp
        OHG[bucket[rel + 639], 0, tp] = 8.0
        if r < 3:
            OHG[:, 1, :] = OHG[:, 0, :]
        else:
            OHG[15, 1, :1279] = 8.0
        if r > 0:
            OHG[:, 2, :] = OHG[:, 0, :]
        else:
            OHG[31, 2, :1279] = 8.0
        d["OHG"] = OHG
        OHC = np.zeros((32, NCOL), np.float32)
        for j in range(16, 64):
            OHC[31 if j < 64 - 16 * r else 15, j] = 1.0
        OHC[15, 64] = 1.0
        OHC[31, 65] = 1.0
        d["OHC"] = OHC
        per_core.append(d)
    return c, per_core


_CACHE = {}


def kernel(x_prompt, x_sample, p_prompt, p_sample, norm_mix_g, w_in, lambda_params, subln_g, rel_bias,
           w_fourier, w_out, ple_norm_g, w_ple_gate, w_ple_proj, final_norm_g):
    f = lambda a: np.ascontiguousarray(np.asarray(a, dtype=np.float32))
    x_prompt, x_sample, p_prompt, p_sample = f(x_prompt), f(x_sample), f(p_prompt), f(p_sample)
    if "nc" not in _CACHE:
        _CACHE["nc"] = build_program()
        _CACHE["consts"] = _host_consts()
    nc = _CACHE["nc"]
    cglob, cper = _CACHE["consts"]
    shared = dict(win=f(w_in)[0], wout=f(w_out)[0], wgate=f(w_ple_gate)[0], wproj=f(w_ple_proj)[0], wf=f(w_fourier)[0],
                  gmix=f(norm_mix_g).reshape(1, D), gple=f(ple_norm_g).reshape(1, D), gfin=f(final_norm_g).reshape(1, D),
                  gsub=f(subln_g).reshape(1, 128), lamp=f(lambda_params).reshape(1, 256), relb=f(rel_bias))
    shared.update(cglob)
    in_maps = []
    for core in range(NCORES):
        b, r = core // 4, core % 4
        m = dict(shared)
        m.update(cper[core])
        m["xp"] = x_prompt[core]
        m["pp"] = p_prompt[0, core]
        m["xs"] = np.ascontiguousarray(np.roll(x_sample[b], -2048 * r, axis=0))
        m["psm"] = np.ascontiguousarray(p_sample[0, b, 2048 * r:2048 * (r + 1)])
        in_maps.append(m)
    if _CACHE.get("prep_only"):
        return in_maps
    res = run_bass_kernel_spmd(nc, in_maps, core_ids=list(range(NCORES)))
    yp = np.stack([res.results[c]["yp"] for c in range(NCORES)], 0)
    ys = np.zeros((2, 8192, D), np.float32)
    for core in range(NCORES):
        b, r = core // 4, core % 4
        ys[b, 2048 * r:2048 * (r + 1)] = res.results[core]["ys"]
    return (yp.astype(np.float32), ys)
```

```python
import contextlib
import math
import numpy as np
import ml_dtypes
import concourse.bass as bass
import concourse.mybir as mybir
from concourse.bass_utils import run_bass_kernel_spmd

F32 = mybir.dt.float32
BF16 = mybir.dt.bfloat16
AF = mybir.ActivationFunctionType
ALU = mybir.AluOpType
AX = mybir.AxisListType

D = 1024
NCORES = 8
JOBS = [dict(NKV=4096, NQ=4096, N2=32, R=4, NK2=32), dict(NKV=8192, NQ=2048, N2=64, R=2, NK2=16)]
NCOL = 67


class Buf:
    __slots__ = ("w", "r", "const")

    def __init__(self, const=False):
        self.w = {}
        self.r = {}
        self.const = const


def _merge(d, s):
    for k, v in s.items():
        if d.get(k, -1) < v:
            d[k] = v


class Rec:
    ENGS = ("pe", "act", "dve", "pool", "sp")

    def __init__(self, nc, stack):
        self.nc = nc
        self.stack = stack
        self.sems = {e: stack.enter_context(nc.semaphore("s_" + e)) for e in self.ENGS if e != "sp"}
        self.base_cnt = {e: 0 for e in self.ENGS}
        self.dcnt = {}
        self.reset()

    def reset(self):
        self.ops = {e: [] for e in self.ENGS}

    def op(self, eng, fn, reads=(), writes=(), dma_key=None):
        deps = {}
        for b in reads:
            _merge(deps, b.w)
        for b in writes:
            _merge(deps, b.r)
            _merge(deps, b.w)
        lst = self.ops[eng]
        idx = len(lst)
        if dma_key is None:
            tk = {eng: idx}
        else:
            n = self.dcnt.get(dma_key, 0) + 16
            self.dcnt[dma_key] = n
            tk = {"d:" + dma_key: n}
            if ("d:" + dma_key) not in self.sems:
                self.sems["d:" + dma_key] = self.stack.enter_context(self.nc.semaphore("d_" + dma_key))
        lst.append((fn, deps, dma_key))
        for b in reads:
            if not b.const:
                _merge(b.r, tk)
        for b in writes:
            b.w = dict(tk)
            b.r = {}
        return tk

    def emit(self, bufs_to_clear=()):
        nc = self.nc
        needed = {e: set() for e in self.ENGS}
        for e in self.ENGS:
            for (fn, deps, dk) in self.ops[e]:
                for ch, v in deps.items():
                    if ch in needed:
                        if ch == "pe" and e == "pe":
                            continue
                        needed[ch].add(v)
            if e != "sp":
                for i in range(len(self.ops[e]) - 1, -1, -1):
                    if self.ops[e][i][2] is None:
                        needed[e].add(i)
                        break
        cum = {}
        for e in self.ENGS:
            c = self.base_cnt[e]
            arr = []
            nd = needed[e]
            for i in range(len(self.ops[e])):
                if i in nd and self.ops[e][i][2] is None:
                    c += 1
                arr.append(c)
            cum[e] = arr
        final = {e: (cum[e][-1] if cum[e] else self.base_cnt[e]) for e in self.ENGS if e != "sp"}
        dfinal = {"d:" + k: v for k, v in self.dcnt.items()}
        sems = self.sems
        with nc.Block() as block:
            handles = {"pe": block.tensor, "act": block.scalar, "dve": block.vector,
                       "pool": block.gpsimd, "sp": block.sync}

            def run(e):
                def body(h):
                    seen = {}
                    for i, (fn, deps, dk) in enumerate(self.ops[e]):
                        for ch, v in deps.items():
                            if ch in cum:
                                if ch == "pe" and e == "pe":
                                    continue
                                val = cum[ch][v]
                            else:
                                val = v
                            if seen.get(ch, 0) < val:
                                h.wait_ge(sems[ch], val)
                                seen[ch] = val
                        inst = fn(h)
                        if dk is not None:
                            inst.then_inc(sems["d:" + dk], 16)
                        elif i in needed[e]:
                            inst.then_inc(sems[e], 1)
                    for ch, val in list(final.items()) + list(dfinal.items()):
                        if ch == e:
                            continue
                        if val > 0 and seen.get(ch, 0) < val:
                            h.wait_ge(sems[ch], val)
                return body

            for e in self.ENGS:
                handles[e](run(e))
        for e in final:
            self.base_cnt[e] = final[e]
        self.reset()
        for b in bufs_to_clear:
            b.w = {}
            b.r = {}


class BufSet:
    def __init__(self):
        self.d = {}

    def __getitem__(self, k):
        b = self.d.get(k)
        if b is None:
            b = Buf()
            self.d[k] = b
        return b

    def clear(self):
        self.d = {}


def build_program(dbg=None, njobs=2):
    nc = bass.Bass("TRN2", target_bir_lowering=False)
    dt_in = lambda n, s, d=F32: nc.dram_tensor(n, list(s), d, kind="ExternalInput")
    xj = [dt_in("xp", [4096, D]), dt_in("xs", [8192, D])]
    pj = [dt_in("pp", [4096, 256]), dt_in("psm", [2048, 256])]
    yj = [nc.dram_tensor("yp", [4096, D], F32, kind="ExternalOutput"),
          nc.dram_tensor("ys", [2048, D], F32, kind="ExternalOutput")]
    win = dt_in("win", [D, 3072]); wout = dt_in("wout", [D, D]); wgate = dt_in("wgate", [D, D])
    wproj = dt_in("wproj", [256, D]); wf = dt_in("wf", [4, 128, 128])
    gmix_d = dt_in("gmix", [1, D]); gple_d = dt_in("gple", [1, D]); gfin_d = dt_in("gfin", [1, D])
    gsub_d = dt_in("gsub", [1, 128]); lam_d = dt_in("lamp", [1, 256]); relb_d = dt_in("relb", [32, 4])
    ident_d = dt_in("ident", [128, 128], BF16); J_d = dt_in("Jm", [128, 128]); CS_d = dt_in("CS", [128, 256])
    F1_d = dt_in("F1ab", [128, 512], BF16)
    tw_d = [dt_in("tw0", [128, 3, 128]), dt_in("tw1", [128, 3, 128])]
    F2_d = [dt_in("F20", [128, 2, 32], BF16), dt_in("F21", [128, 2, 32], BF16)]
    OHG_d = dt_in("OHG", [32, 3, 1280]); OHC_d = dt_in("OHC", [32, NCOL])
    kw = dict(kind="ExternalOutput") if dbg else {}
    Gd = nc.dram_tensor("Gd", [3, 4, 1280], F32, **kw)
    KTd = nc.dram_tensor("KTd", [4, 128, 8192], BF16, **kw)
    QTd = nc.dram_tensor("QTd", [4, 128, 4096], BF16, **kw)
    UTd = nc.dram_tensor("UTd", [4, 128, 8192], BF16, **kw)
    Vd = nc.dram_tensor("Vd", [4, 128, 64, 129], BF16, **kw)
    G2d = nc.dram_tensor("G2d", [4, 128, 32, 128], BF16, **kw)
    GFd = nc.dram_tensor("GFd", [4, 128, 32, 128], BF16, **kw)
    if dbg:
        dcb = nc.dram_tensor("dcb", [128, 4 * NCOL], F32, kind="ExternalOutput")
        dbt = nc.dram_tensor("dbt", [128, 4 * 1024], BF16, kind="ExternalOutput")
        dcsw = nc.dram_tensor("dcsw", [128, 4 * 256], BF16, kind="ExternalOutput")
        dlam = nc.dram_tensor("dlam", [128, 4], F32, kind="ExternalOutput")
        dmix = nc.dram_tensor("dmix", [128, 8 * 4096], BF16, kind="ExternalOutput")

    with contextlib.ExitStack() as G:
        _cnt = [0]

        def sbt(st, n, s, d):
            _cnt[0] += 1
            return st.enter_context(nc.sbuf_tensor("sb%d_%s" % (_cnt[0], n), list(s), d))
        R = Rec(nc, G)
        B = BufSet()
        ps = G.enter_context(nc.psum_tensor("ps", [128, 8, 512], F32))

        def psb(bank):
            return ps[:, bank, :].bitcast(BF16)

        ident = sbt(G, "ident", [128, 128], BF16)
        Jf = sbt(G, "Jf", [128, 128], F32)
        F1ab = sbt(G, "F1ab", [128, 512], BF16)
        tw = [sbt(G, "tw0", [128, 3, 128], F32), sbt(G, "tw1", [128, 3, 128], F32)]
        twb = [sbt(G, "twb0", [128, 2, 2, 128], BF16), sbt(G, "twb1", [128, 2, 2, 128], BF16)]
        F2 = [sbt(G, "F20", [128, 2, 32], BF16), sbt(G, "F21", [128, 2, 32], BF16)]
        cb = sbt(G, "cb", [128, 4, NCOL], F32)
        lamc = sbt(G, "lamc", [128, 4], F32)
        gsub2 = sbt(G, "gsub2", [128, 128], F32)
        BTw = sbt(G, "BTw", [128, 4, 1024], BF16)
        cb8 = sbt(G, "cb8", [128, 4, NCOL], F32)
        CSW = sbt(G, "CSW", [128, 4, 256], BF16)
        gvec = sbt(G, "gvec", [128, 3, D], F32)
        epsc = sbt(G, "epsc", [128, 2], F32)

        def dma(out, in_, key, reads=(), writes=(), eng="sp"):
            return R.op(eng, lambda h: h.dma_start(out=out, in_=in_), reads=reads, writes=writes, dma_key=key)

        with contextlib.ExitStack() as S:
            relb = sbt(S, "relb", [32, 4], F32)
            OHC = sbt(S, "OHC", [32, NCOL], F32)
            OHG = sbt(S, "OHG", [32, 3, 1280], F32)
            rhs4 = sbt(S, "rhs4", [32, 4, NCOL], F32)
            ones32 = sbt(S, "ones32", [32, 128], F32)
            Gsb = sbt(S, "Gsb", [4, 3, 1280], F32)
            Hsb4 = sbt(S, "Hsb", [128, 4, 2176], F32)
            CS = sbt(S, "CS", [128, 256], F32)
            wfs = sbt(S, "wfs", [128, 4, 128], F32)
            lpb = sbt(S, "lpb", [128, 256], F32)
            lpt = sbt(S, "lpt", [128, 2, 64], F32)
            lps = sbt(S, "lps", [128, 2], F32)
            gsb_ = sbt(S, "gsb_", [128, 128], F32)

            dma(ident[:, :], ident_d.ap()[:, :], "c0", writes=[B["ident"]])
            dma(Jf[:, :], J_d.ap()[:, :], "c1", writes=[B["Jf"]])
            dma(F1ab[:, :], F1_d.ap()[:, :], "c2", writes=[B["F1"]])
            for j in range(2):
                dma(tw[j][:, :, :], tw_d[j].ap()[:, :, :], "c3_%d" % j, writes=[B["tw%d" % j]])
                dma(F2[j][:, :, :], F2_d[j].ap()[:, :, :], "c4_%d" % j, writes=[B["F2%d" % j]])
            dma(relb[:, :], relb_d.ap()[:, :], "c5", writes=[B["relb"]])
            dma(OHC[:, :], OHC_d.ap()[:, :], "c6", writes=[B["OHC"]])
            dma(OHG[:, :, :], OHG_d.ap()[:, :, :], "c7", writes=[B["OHG"]])
            dma(CS[:, :], CS_d.ap()[:, :], "c8", writes=[B["CS"]])
            dma(wfs[:, :, :], wf.ap().rearrange("g c d -> c g d"), "c9", writes=[B["wfs"]])
            dma(lpb[:, :], bass.AP(lam_d, 0, [[0, 128], [1, 256]]), "c10", writes=[B["lpb"]])
            dma(gsb_[:, :], bass.AP(gsub_d, 0, [[0, 128], [1, 128]]), "c11", writes=[B["gsb"]])
            for i, gd in enumerate([gmix_d, gple_d, gfin_d]):
                dma(gvec[:, i, :], bass.AP(gd, 0, [[0, 128], [1, D]]), "c12_%d" % i, writes=[B["gvec%d" % i]])
            for j in range(2):
                R.op("dve", lambda h, j=j: h.tensor_copy(out=twb[j][:, 0, :, :], in_=tw[j][:, 0:2, :]), reads=[B["tw%d" % j]], writes=[B["twb%d" % j]])
                R.op("dve", lambda h, j=j: h.tensor_copy(out=twb[j][:, 1, :, :], in_=tw[j][:, 1:3, :]), reads=[B["tw%d" % j]], writes=[B["twb%d" % j]])
            R.op("pool", lambda h: h.memset(epsc[:, 0:1], 1e-6), writes=[B["eps"]])
            R.op("pool", lambda h: h.memset(epsc[:, 1:2], 1e-5), writes=[B["eps"]])
            R.op("pool", lambda h: h.memset(ones32[:, :], 1.0), writes=[B["ones32"]])
            R.op("pool", lambda h: h.tensor_scalar(out=gsub2[:, :], in0=gsb_[:, :], scalar1=0.8, scalar2=None, op0=ALU.mult),
                 reads=[B["gsb"]], writes=[B["gsub2"]])
            lpv = lpb[:, :].rearrange("p (a b c) -> p a b c", a=2, b=2)
            R.op("dve", lambda h: h.tensor_tensor(out=lpt[:, :, :], in0=lpv[:, :, 0, :], in1=lpv[:, :, 1, :], op=ALU.mult),
                 reads=[B["lpb"]], writes=[B["lpt"]])
            R.op("dve", lambda h: h.tensor_reduce(out=lps[:, :], in_=lpt[:, :, :], axis=AX.X, op=ALU.add),
                 reads=[B["lpt"]], writes=[B["lps"]])
            R.op("act", lambda h: h.activation(out=lamc[:, 0:2], in_=lps[:, :], func=AF.Exp), reads=[B["lps"]], writes=[B["lamc"]])
            R.op("dve", lambda h: h.tensor_tensor(out=lamc[:, 2:3], in0=lamc[:, 0:1], in1=lamc[:, 1:2], op=ALU.subtract),
                 reads=[B["lamc"]], writes=[B["lamc"]])
            R.op("dve", lambda h: h.tensor_scalar(out=lamc[:, 3:4], in0=lamc[:, 2:3], scalar1=0.2, scalar2=-1.0, op0=ALU.add, op1=ALU.mult),
                 reads=[B["lamc"]], writes=[B["lamc"]])
            for hh in range(4):
                R.op("dve", lambda h, hh=hh: h.tensor_scalar(out=rhs4[:, hh, :], in0=OHC[:, :], scalar1=relb[:, hh:hh + 1], scalar2=None, op0=ALU.mult),
                     reads=[B["OHC"], B["relb"]], writes=[B["rhs4"]])
            R.op("pe", lambda h: h.matmul(ps[:, 0, 0:4 * NCOL], lhsT=ones32[:, :], rhs=rhs4[:, :, :].rearrange("p a b -> p (a b)"), start=True, stop=True),
                 reads=[B["ones32"], B["rhs4"]], writes=[B["ps0"]])
            R.op("dve", lambda h: h.tensor_copy(out=cb[:, :, :].rearrange("p a b -> p (a b)"), in_=ps[:, 0, 0:4 * NCOL]), reads=[B["ps0"]], writes=[B["cb"]])
            R.op("dve", lambda h: h.tensor_scalar(out=cb8[:, :, :].rearrange("p a b -> p (a b)"), in0=cb[:, :, :].rearrange("p a b -> p (a b)"),
                                                  scalar1=8.0, scalar2=None, op0=ALU.mult), reads=[B["cb"]], writes=[B["cb8"]])
            bank_of = {}
            bi = 1
            for v in range(3):
                for (c0, c1) in [(0, 512), (512, 1024), (1024, 1280)]:
                    bnk = 1 + (bi - 1) % 7
                    bi += 1
                    R.op("pe", lambda h, v=v, c0=c0, c1=c1, bnk=bnk: h.matmul(ps[0:4, bnk, 0:c1 - c0], lhsT=relb[:, :], rhs=OHG[:, v, c0:c1], start=True, stop=True),
                         reads=[B["relb"], B["OHG"]], writes=[B["psb%d" % bnk]])
                    R.op("dve", lambda h, v=v, c0=c0, c1=c1, bnk=bnk: h.tensor_copy(out=Gsb[:, v, c0:c1], in_=ps[0:4, bnk, 0:c1 - c0]),
                         reads=[B["psb%d" % bnk]], writes=[B["Gsb"]])
            dma(Gd.ap().rearrange("v h t -> h v t"), Gsb[:, :, :], "c13", reads=[B["Gsb"]], writes=[B["Gd"]])
            for hh in range(4):
                dma(Hsb4[:, hh, 0:1152], bass.AP(Gd, (0 * 4 + hh) * 1280, [[1, 128], [1, 1152]]), "c14_%d" % hh, reads=[B["Gd"]], writes=[B["Hsb%d" % hh]])
                dma(Hsb4[:, hh, 1152:1664], bass.AP(Gd, (1 * 4 + hh) * 1280, [[1, 128], [1, 512]]), "c14_%d" % hh, reads=[B["Gd"]], writes=[B["Hsb%d" % hh]])
                dma(Hsb4[:, hh, 1664:2176], bass.AP(Gd, (2 * 4 + hh) * 1280 + 640, [[1, 128], [1, 512]]), "c14_%d" % hh, reads=[B["Gd"]], writes=[B["Hsb%d" % hh]])
            for hh in range(4):
                bA, bB = 1 + 2 * (hh % 2), 2 + 2 * (hh % 2)
                R.op("pe", lambda h, hh=hh, bA=bA: h.matmul(ps[:, bA, 0:384], lhsT=Jf[:, :], rhs=Hsb4[:, hh, 384:768], start=True, stop=True),
                     reads=[B["Jf"], B["Hsb%d" % hh]], writes=[B["psb%d" % bA]])
                R.op("pe", lambda h, hh=hh, bA=bA: h.matmul(ps[:, bA, 384:512], lhsT=Jf[:, :], rhs=Hsb4[:, hh, 1152 + 384:1152 + 512], start=True, stop=True),
                     reads=[B["Jf"], B["Hsb%d" % hh]], writes=[B["psb%d" % bA]])
                R.op("pe", lambda h, hh=hh, bB=bB: h.matmul(ps[:, bB, 0:128], lhsT=Jf[:, :], rhs=Hsb4[:, hh, 1664:1792], start=True, stop=True),
                     reads=[B["Jf"], B["Hsb%d" % hh]], writes=[B["psb%d" % bB]])
                for (dst0, n, bank, src0, col) in [(0, 384, bA, 0, 64), (384, 384, bA, 0, 65), (768, 128, bA, 384, 16), (896, 128, bB, 0, 63)]:
                    R.op("dve", lambda h, hh=hh, dst0=dst0, n=n, bank=bank, src0=src0, col=col: h.tensor_scalar(
                        out=BTw[:, hh, dst0:dst0 + n], in0=ps[:, bank, src0:src0 + n], scalar1=cb8[:, hh, col:col + 1], scalar2=None, op0=ALU.subtract),
                         reads=[B["psb%d" % bank], B["cb8"]], writes=[B["BTw"]])
            for g in range(4):
                for q in range(2):
                    R.op("pe", lambda h, g=g, q=q: h.matmul(ps[:, 6, q * 128:(q + 1) * 128], lhsT=CS[:, q * 128:(q + 1) * 128], rhs=wfs[:, g, :], start=True, stop=True),
                         reads=[B["CS"], B["wfs"]], writes=[B["psb6"]])
                R.op("dve", lambda h, g=g: h.tensor_copy(out=CSW[:, g, :], in_=ps[:, 6, 0:256]), reads=[B["psb6"]], writes=[B["CSW"]])
            R.emit()
            B.clear()
            if dbg:
                dma(dcb.ap()[:, :], cb[:, :, :].rearrange("p a b -> p (a b)"), "g0")
                dma(dbt.ap()[:, :], BTw[:, :, :].rearrange("p a b -> p (a b)"), "g1")
                dma(dcsw.ap()[:, :], CSW[:, :, :].rearrange("p a b -> p (a b)"), "g2")
                dma(dlam.ap()[:, :], lamc[:, :], "g3")
                R.emit()
                if dbg == "S":
                    return nc

        def load_weight(wst, dst, src_ap, nch, ncols, name, cw=1024):
            for c in range(nch):
                dma(dst[:, c, 0:ncols], src_ap[c * 128:(c + 1) * 128, 0:ncols], "%s_%d" % (name, c), eng="pool", writes=[B[name + "_%d" % c]])

        def rstd_ops(ss_ap, out_ap, scale, eps_ap, tag, bufs_r, buf_w, tmp_ap):
            R.op("act", lambda h: h.activation(out=tmp_ap, in_=ss_ap, func=AF.Ln, bias=eps_ap, scale=scale),
                 reads=bufs_r + [B["epsc"]], writes=[B[tag + "_t"]])
            R.op("act", lambda h: h.activation(out=out_ap, in_=tmp_ap, func=AF.Exp, scale=-0.5),
                 reads=[B[tag + "_t"]], writes=[buf_w])

        for ji, job in enumerate(JOBS[:njobs]):
            NKV, NQ, N2, RR, NK2 = job["NKV"], job["NQ"], job["N2"], job["R"], job["NK2"]
            NB = NKV // 512
            NBQ = NQ // 512
            x_ap = xj[ji].ap()
            with contextlib.ExitStack() as A:
                Wb = sbt(A, "Wb", [128, 8, 3072], BF16)
                load_weight(None, Wb, win.ap(), 8, 3072, "Wb")
                xt = sbt(A, "xt", [128, 4, D], F32)
                junk = sbt(A, "junk", [128, D], BF16)
                xnb = sbt(A, "xnb", [128, 4, D], BF16)
                xnT = sbt(A, "xnT", [128, 2, 8, 512], BF16)
                ssA = sbt(A, "ssA", [128, 3, 3, 4], F32)
                FMst = sbt(A, "FMst", [128, 2, 12, 512], BF16)
                TMst = sbt(A, "TMst", [128, 2, 4, 4, 129], BF16)
                G2st = sbt(A, "G2st", [128, 2, 4, 4, 128], BF16)
                GFst = sbt(A, "GFst", [128, 2, 4, 4, 128], BF16)
                g2f = sbt(A, "g2f", [128, 2, 512], F32)
                for s in range(2):
                    R.op("pool", lambda h, s=s: h.memset(TMst[:, s, :, :, 128:129], 1.0), writes=[B["TMst%d" % s]])

                def FE1_items(blk):
                    b3 = blk % 3
                    items = []
                    items.append(lambda: R.op("dve", lambda h, b3=b3: h.memset(ssA[:, b3, 0, :], 0.0), writes=[B["ss%d" % b3]]))
                    for ti in range(4):
                        items.append(lambda ti=ti: R.op("act", lambda h, b3=b3, ti=ti: h.activation(out=junk[:, :], in_=xt[:, ti, :], func=AF.Square,
                                                                                            accum_out=ssA[:, b3, 0, ti:ti + 1]),
                                                        reads=[B["xt%d" % ti]], writes=[B["junk"], B["ss%d" % b3]]))
                    items.append(lambda: rstd_ops(ssA[:, b3, 0, :], ssA[:, b3, 2, :], 1.0 / D, epsc[:, 0:1], "rA%d" % b3, [B["ss%d" % b3]], B["rs%d" % b3], ssA[:, b3, 1, :]))
                    for ti in range(4):
                        items.append(lambda ti=ti: R.op("dve", lambda h, ti=ti, b3=b3: h.scalar_tensor_tensor(out=xnb[:, ti, :], in0=xt[:, ti, :], scalar=ssA[:, b3, 2, ti:ti + 1],
                                                                                                   in1=gvec[:, 0, :], op0=ALU.mult, op1=ALU.mult),
                                                        reads=[B["xt%d" % ti], B["rs%d" % b3]], writes=[B["xnb%d" % ti]]))
                    return items

                def FE1(blk):
                    for it in FE1_items(blk):
                        it()

                def FE2_tile(blk, ti):
                    bs = blk % 2
                    tb = (0, 7)[ti % 2]
                    for c in range(8):
                        R.op("pe", lambda h, ti=ti, c=c, tb=tb: h.transpose(out=psb(tb)[:, c * 128:(c + 1) * 128], in_=xnb[:, ti, c * 128:(c + 1) * 128], identity=ident[:, :]),
                             reads=[B["xnb%d" % ti]], writes=[B["pb%d" % tb]])
                    if ti % 2 == 0:
                        R.op("act", lambda h, bs=bs, ti=ti, tb=tb: h.activation(out=xnT[:, bs, :, ti * 128:(ti + 1) * 128],
                                                                                in_=psb(tb).rearrange("p (c t) -> p c t", c=8), func=AF.Copy),
                             reads=[B["pb%d" % tb]], writes=[B["xnT%d" % bs]])
                    else:
                        R.op("dve", lambda h, bs=bs, ti=ti, tb=tb: h.tensor_copy(out=xnT[:, bs, :, ti * 128:(ti + 1) * 128],
                                                                                 in_=psb(tb).rearrange("p (c t) -> p c t", c=8)),
                             reads=[B["pb%d" % tb]], writes=[B["xnT%d" % bs]])

                def FE2(blk):
                    for ti in range(4):
                        FE2_tile(blk, ti)

                def loads(blk):
                    for ti in range(4):
                        t = blk * 4 + ti
                        dma(xt[:, ti, :], x_ap[t * 128:(t + 1) * 128, :], "x%d" % ti, writes=[B["xt%d" % ti]])

                fmk = [0]

                def TM_tile(blk, ti):
                    mine = blk < NBQ
                    bs = blk % 2
                    vb = 3 + 3 * (ti % 2)
                    for c in range(8):
                        R.op("pe", lambda h, bs=bs, ti=ti, c=c, vb=vb: h.matmul(ps[:, vb, :], lhsT=xnT[:, bs, c, ti * 128:(ti + 1) * 128], rhs=Wb[:, c, 1024:1536],
                                                                                 start=(c == 0), stop=(c == 7)),
                             reads=[B["xnT%d" % bs], B["Wb_%d" % c]], writes=[B["pb%d" % vb]])
                    R.op("dve", lambda h, bs=bs, ti=ti, vb=vb: h.tensor_copy(out=TMst[:, bs, :, ti, 0:128], in_=ps[:, vb, :].rearrange("p (a b) -> p a b", a=4)),
                         reads=[B["pb%d" % vb]], writes=[B["TMst%d" % bs]])
                    if mine:
                        gb = 4
                        gs = ti % 2
                        for c in range(8):
                            R.op("pe", lambda h, bs=bs, ti=ti, c=c, gb=gb: h.matmul(ps[:, gb, :], lhsT=xnT[:, bs, c, ti * 128:(ti + 1) * 128],
                                                                                  rhs=Wb[:, c, 1536:2048], start=(c == 0), stop=(c == 7)),
                                 reads=[B["xnT%d" % bs], B["Wb_%d" % c]], writes=[B["pb%d" % gb]])
                        R.op("act", lambda h, gb=gb, gs=gs: h.activation(out=g2f[:, gs, :], in_=ps[:, gb, :], func=AF.Silu),
                             reads=[B["pb%d" % gb]], writes=[B["g2f%d" % gs]])
                        R.op("dve", lambda h, bs=bs, ti=ti, gs=gs: h.tensor_tensor(out=G2st[:, bs, :, ti, :], in0=g2f[:, gs, :].rearrange("p (a b) -> p a b", a=4),
                                                                                  in1=gsub2[:, :].unsqueeze(1).to_broadcast([128, 4, 128]), op=ALU.mult),
                             reads=[B["g2f%d" % gs]], writes=[B["G2st%d" % bs]])
                        for c in range(8):
                            R.op("pe", lambda h, bs=bs, ti=ti, c=c: h.matmul(ps[:, 5, :], lhsT=xnT[:, bs, c, ti * 128:(ti + 1) * 128],
                                                                           rhs=Wb[:, c, 2560:3072], start=(c == 0), stop=(c == 7)),
                                 reads=[B["xnT%d" % bs], B["Wb_%d" % c]], writes=[B["pb5"]])
                        R.op("act", lambda h, bs=bs, ti=ti: h.activation(out=GFst[:, bs, :, ti, :], in_=ps[:, 5, :].rearrange("p (a b) -> p a b", a=4), func=AF.Silu),
                             reads=[B["pb5"]], writes=[B["GFst%d" % bs]])

                def FM(blk, items):
                    mine = blk < NBQ
                    bs = blk % 2
                    chunks = [(512 + hh * 128, hh) for hh in range(4)] + [(2048 + g * 128, 4 + g) for g in range(4)]
                    if mine:
                        chunks += [(hh * 128, 8 + hh) for hh in range(4)]
                    items = list(items)
                    for ci, (c0, idx) in enumerate(chunks):
                        fb = 1 + fmk[0] % 2
                        for c in range(8):
                            R.op("pe", lambda h, bs=bs, c=c, c0=c0, fb=fb: h.matmul(ps[:, fb, :], lhsT=Wb[:, c, c0:c0 + 128], rhs=xnT[:, bs, c, :], start=(c == 0), stop=(c == 7)),
                                 reads=[B["xnT%d" % bs], B["Wb_%d" % c]], writes=[B["pb%d" % fb]])
                        if (fmk[0] % 2 == 0) if mine else (fmk[0] % 3 != 2):
                            R.op("act", lambda h, bs=bs, idx=idx, fb=fb: h.activation(out=FMst[:, bs, idx, :], in_=ps[:, fb, :], func=AF.Copy),
                                 reads=[B["pb%d" % fb]], writes=[B["FMst%d" % bs]])
                        else:
                            R.op("dve", lambda h, bs=bs, idx=idx, fb=fb: h.tensor_copy(out=FMst[:, bs, idx, :], in_=ps[:, fb, :]),
                                 reads=[B["pb%d" % fb]], writes=[B["FMst%d" % bs]])
                        fmk[0] += 1
                        rem_chunks = len(chunks) - ci
                        ntake = -(-len(items) // rem_chunks)
                        for _ in range(ntake):
                            items.pop(0)()
                    tsl = slice(blk * 512, (blk + 1) * 512)
                    dma(KTd.ap()[:, :, tsl].rearrange("h p t -> p h t"), FMst[:, bs, 0:4, :], "oK%d" % bs, eng="pool", reads=[B["FMst%d" % bs]])
                    dma(UTd.ap()[:, :, tsl].rearrange("h p t -> p h t"), FMst[:, bs, 4:8, :], "oU%d" % bs, eng="pool", reads=[B["FMst%d" % bs]])
                    dma(Vd.ap()[:, :, blk * 4:(blk + 1) * 4, :].rearrange("h p t e -> p h t e"), TMst[:, bs, :, :, :], "oV%d" % bs, eng="pool", reads=[B["TMst%d" % bs]])
                    if mine:
                        dma(QTd.ap()[:, :, tsl].rearrange("h p t -> p h t"), FMst[:, bs, 8:12, :], "oQ%d" % bs, eng="pool", reads=[B["FMst%d" % bs]])
                        dma(G2d.ap()[:, :, blk * 4:(blk + 1) * 4, :].rearrange("h p t e -> p h t e"), G2st[:, bs, :, :, :], "oG%d" % bs, eng="pool", reads=[B["G2st%d" % bs]])
                        dma(GFd.ap()[:, :, blk * 4:(blk + 1) * 4, :].rearrange("h p t e -> p h t e"), GFst[:, bs, :, :, :], "oF%d" % bs, eng="pool", reads=[B["GFst%d" % bs]])

                loads(0)
                FE1(0)
                loads(1)
                FE2(0)
                FE1(1)
                loads(2)
                for blk in range(NB):
                    for ti in range(4):
                        if blk + 1 < NB:
                            FE2_tile(blk + 1, ti)
                        TM_tile(blk, ti)
                    items = FE1_items(blk + 2) if blk + 2 < NB else []
                    FM(blk, items)
                    if blk + 3 < NB:
                        loads(blk + 3)
                R.emit()
                B.clear()
            if dbg == "A":
                return nc

            with contextlib.ExitStack() as M:
                mixT = sbt(M, "mixT", [128, 8, NQ], BF16)
                with contextlib.ExitStack() as P:
                    KT2 = sbt(P, "KT", [128, 2, NKV], BF16)
                    QT2 = sbt(P, "QT", [128, 2, NQ], BF16)
                    Vh2 = sbt(P, "Vh", [128, 2, NKV // 128, 129], BF16)
                    G2h2 = sbt(P, "G2h", [128, 2, NQ // 128, 128], BF16)
                    PT = sbt(P, "PT", [128, 3, 2, 512], BF16)
                    stg = sbt(P, "stg", [128, 8, 129], F32)
                    r8 = sbt(P, "r8", [128, 8], F32)
                    o4 = sbt(P, "o4", [128, 4, 128], F32)
                    sq4 = sbt(P, "sq4", [128, 4, 128], F32)
                    ss4 = sbt(P, "ss4", [128, 3, 4], F32)
                    attb = sbt(P, "attb", [128, 4, 128], BF16)
                    NJ = NKV // 128
                    NM = NQ // 512

                    def near_info(m, j):
                        if ji == 0 or j <= 15:
                            tau = j - 4 * m
                            if -1 <= tau <= 4:
                                i0, i1 = max(0, tau - 1), min(3, tau + 1)
                                base = 0 if tau <= 1 else 384
                                return (64 if tau <= 1 else 65), (base + (1 - tau + i0) * 128, i0, i1 - i0 + 1)
                            return (64 if tau < -1 else 65), None
                        if j == 16 and m == 3:
                            return 16, (768, 3, 1)
                        if j == 63 and m == 0:
                            return 63, (896, 0, 1)
                        return j, None

                    step_id = 0
                    pend = [None]
                    def loadsB(hh):
                        hp = hh % 2
                        dma(QT2[:, hp, 0:NQ // 2], QTd.ap()[hh, :, 0:NQ // 2], "lQ%d_0" % hp, writes=[B["QT%d_0" % hp]])
                        for ci in range(4):
                            k0, k1 = ci * (NKV // 4), (ci + 1) * (NKV // 4)
                            dma(KT2[:, hp, k0:k1], KTd.ap()[hh, :, k0:k1], "lK%d_%d" % (hp, ci), writes=[B["KT%d_%d" % (hp, ci)]])
                            dma(Vh2[:, hp, k0 // 128:k1 // 128, :], Vd.ap()[hh, :, k0 // 128:k1 // 128, :], "lV%d_%d" % (hp, ci), writes=[B["Vh%d_%d" % (hp, ci)]])
                        dma(QT2[:, hp, NQ // 2:NQ], QTd.ap()[hh, :, NQ // 2:NQ], "lQ%d_1" % hp, writes=[B["QT%d_1" % hp]])
                        dma(G2h2[:, hp, :, :], G2d.ap()[hh, :, 0:NQ // 128, :], "lG%d" % hp, writes=[B["G2h%d" % hp]])

                    loadsB(0)
                    for hh in range(4):
                        hp = hh % 2
                        if pend[0] is not None:
                            for it in pend[0]:
                                it[0]()
                            pend[0] = None
                        if hh + 1 < 4:
                            loadsB(hh + 1)
                        KT = KT2[:, hp, :]
                        QT = QT2[:, hp, :]
                        Vh = Vh2[:, hp, :, :]
                        G2h = G2h2[:, hp, :, :]
                        bG2h = B["G2h%d" % hp]
                        JQ = NJ // 4
                        steps = [(m, j) for m in range(NM) for j in range(NJ)]

                        def QK(si, m, j):
                            slot = si % 2
                            col, off = near_info(m, j)
                            for c in range(2):
                                R.op("pe", lambda h, c=c, j=j, m=m, slot=slot, off=off, KT=KT, QT=QT: h.matmul(ps[:, slot * 2 + c, :], lhsT=KT[64 * c:64 * c + 64, j * 128:(j + 1) * 128],
                                                                                                 rhs=QT[64 * c:64 * c + 64, m * 512:(m + 1) * 512], start=True, stop=(off is None)),
                                     reads=[B["KT%d_%d" % (hp, j // JQ)], B["QT%d_%d" % (hp, (2 * m) // NM)]], writes=[B["S%d" % slot]])
                            if off is not None:
                                for c in range(2):
                                    R.op("pe", lambda h, c=c, slot=slot, off=off, hh=hh: h.matmul(ps[:, slot * 2 + c, off[1] * 128:(off[1] + off[2]) * 128], lhsT=ident[:, :],
                                                                                                  rhs=BTw[:, hh, off[0]:off[0] + off[2] * 128], start=False, stop=True),
                                         reads=[], writes=[B["S%d" % slot]])
                            return col

                        cols = {}
                        cols[0] = QK(step_id, *steps[0])
                        for k, (m, j) in enumerate(steps):
                            si = step_id + k
                            if k + 1 < len(steps):
                                cols[k + 1] = QK(si + 1, *steps[k + 1])
                            slot = si % 2
                            p3 = si % 3
                            col = cols.pop(k)
                            R.op("act", lambda h, slot=slot, p3=p3, col=col, hh=hh: h.activation(out=PT[:, p3, :, :], in_=ps[:, slot * 2:slot * 2 + 2, :], func=AF.Exp,
                                                                                                 bias=cb[:, hh, col:col + 1], scale=0.125),
                                 reads=[B["S%d" % slot]], writes=[B["PT%d" % p3]])
                            for c in range(2):
                                for i in range(4):
                                    idx = c * 4 + i
                                    R.op("pe", lambda h, p3=p3, c=c, i=i, idx=idx, j=j, Vh=Vh: h.matmul(ps[:, 4 + idx // 3, (idx % 3) * 129:(idx % 3) * 129 + 129],
                                                                                               lhsT=PT[:, p3, c, i * 128:(i + 1) * 128], rhs=Vh[:, j, :],
                                                                                               start=(j == 0 and idx % 3 == 0), stop=(j == NJ - 1),
                                                                                               skip_group_check=True),
                                         reads=[B["PT%d" % p3], B["Vh%d_%d" % (hp, j // JQ)]], writes=[B["acc%d" % (4 + idx // 3)]])
                            if j == NJ - 1:
                                for bk, (a0, n) in enumerate([(0, 3), (3, 3), (6, 2)]):
                                    R.op("dve", lambda h, bk=bk, a0=a0, n=n: h.tensor_copy(out=stg[:, a0:a0 + n, :],
                                                                                          in_=ps[:, 4 + bk, 0:n * 129].rearrange("p (a b) -> p a b", b=129)),
                                         reads=[B["acc%d" % (4 + bk)]], writes=[B["stg%d" % bk]])
                                R.op("dve", lambda h: h.reciprocal(out=r8[:, :], in_=stg[:, :, 128]), reads=[B["stg0"], B["stg1"], B["stg2"]], writes=[B["r8"]])
                                R.op("dve", lambda h: h.tensor_scalar(out=r8[:, 4:8], in0=r8[:, 4:8], scalar1=lamc[:, 3:4], scalar2=None, op0=ALU.mult),
                                     reads=[B["r8"]], writes=[B["r8"]])
                                R.op("dve", lambda h: h.tensor_tensor(out=stg[:, :, 0:128], in0=stg[:, :, 0:128], in1=r8[:, :].unsqueeze(2).to_broadcast([128, 8, 128]), op=ALU.mult),
                                     reads=[B["r8"], B["stg0"], B["stg1"], B["stg2"]], writes=[B["stg0"], B["stg1"], B["stg2"]])
                                R.op("dve", lambda h: h.tensor_tensor(out=o4[:, :, :], in0=stg[:, 0:4, 0:128], in1=stg[:, 4:8, 0:128], op=ALU.add),
                                     reads=[B["stg0"], B["stg1"], B["stg2"]], writes=[B["o4"]])
                                R.op("dve", lambda h: h.tensor_tensor(out=sq4[:, :, :], in0=o4[:, :, :], in1=o4[:, :, :], op=ALU.mult),
                                     reads=[B["o4"]], writes=[B["sq4"]])
                                R.op("dve", lambda h: h.tensor_reduce(out=ss4[:, 0, :], in_=sq4[:, :, :], axis=AX.X, op=ALU.add),
                                     reads=[B["sq4"]], writes=[B["ss4"]])
                                def _tail1(m=m, G2h=G2h, bG2h=bG2h, hh=hh):
                                    rstd_ops(ss4[:, 0, :], ss4[:, 2, :], 1.0 / 128, epsc[:, 1:2], "rB", [B["ss4"]], B["rs4"], ss4[:, 1, :])
                                    R.op("dve", lambda h: h.tensor_tensor(out=o4[:, :, :], in0=o4[:, :, :], in1=ss4[:, 2, :].unsqueeze(2).to_broadcast([128, 4, 128]), op=ALU.mult),
                                         reads=[B["rs4"], B["o4"]], writes=[B["o4"]])
                                    R.op("dve", lambda h, m=m, G2h=G2h: h.tensor_tensor(out=attb[:, :, :], in0=o4[:, :, :], in1=G2h[:, m * 4:(m + 1) * 4, :], op=ALU.mult),
                                         reads=[B["o4"], bG2h], writes=[B["attb"]])

                                def _tail2(m=m, hh=hh):
                                    for i in range(4):
                                        R.op("pe", lambda h, i=i: h.transpose(out=psb(7)[:, i * 128:(i + 1) * 128], in_=attb[:, i, :], identity=ident[:, :]),
                                             reads=[B["attb"]], writes=[B["pb7"]])
                                    R.op("dve", lambda h, hh=hh, m=m: h.tensor_copy(out=mixT[:, hh, m * 512:(m + 1) * 512], in_=psb(7)[:, 0:512]),
                                         reads=[B["pb7"]], writes=[B["mixT"]])
                                pend[0] = [[_tail1, 5], [_tail2, 12]]
                            elif pend[0] is not None:
                                for it in pend[0]:
                                    it[1] -= 1
                                while pend[0] and pend[0][0][1] <= 0:
                                    pend[0].pop(0)[0]()
                                if not pend[0]:
                                    pend[0] = None
                        step_id += len(steps)
                    if pend[0] is not None:
                        for it in pend[0]:
                            it[0]()
                        pend[0] = None
                    R.emit()
                    B.clear()
                if dbg == "B":
                    dma(dmix.ap()[:, 0:8 * NQ], mixT[:, :, :].rearrange("p a b -> p (a b)"), "g4")
                    R.emit()
                    return nc

                with contextlib.ExitStack() as P:
                    uT = sbt(P, "uT", [128, 2, NKV], BF16)
                    GFg = sbt(P, "GFg", [128, 2, NQ // 128, 128], BF16)
                    PQ = sbt(P, "PQ", [128, 2, 128, N2], BF16)
                    mt = sbt(P, "mt", [128, 2, 2, 4, 2, 128], BF16)
                    X2 = sbt(P, "X2", [128, 2, 4, 2, 128], BF16)
                    Ab = sbt(P, "Ab", [128, 2, 4, 2, 128], BF16)
                    fng = sbt(P, "fng", [128, NK2, 128], BF16)
                    NU = 128 // RR
                    CPB = 512 // NK2
                    def loadsC(g):
                        dma(uT[:, g % 2, :], UTd.ap()[g, :, 0:NKV], "lU%d" % (g % 2), writes=[B["uT%d" % (g % 2)]])
                        dma(GFg[:, g % 2, :, :], GFd.ap()[g, :, 0:NQ // 128, :], "lF%d" % (g % 2), writes=[B["GFg%d" % (g % 2)]])

                    loadsC(0)
                    for g in range(4):
                        gp2 = g % 2
                        if g + 1 < 4:
                            loadsC(g + 1)
                        for s2 in range(N2):
                            bnk = (0, 1, 6, 7)[(s2 // 2) % 4]
                            R.op("pe", lambda h, s2=s2, bnk=bnk, g=g: h.matmul(ps[:, bnk, (s2 % 2) * 256:(s2 % 2) * 256 + 256], lhsT=uT[:, g % 2, s2:NKV:N2], rhs=CSW[:, g, :],
                                                                              start=True, stop=True),
                                 reads=[B["uT%d" % (g % 2)]], writes=[B["pb%d" % bnk]])
                            if s2 % 2 == 1:
                                eng = ("act", "dve")[(s2 // 2) % 2]
                                src = ps[:, bnk, :].rearrange("p (s q d) -> p q d s", s=2, q=2)
                                dst = PQ[:, :, :, s2 - 1:s2 + 1]
                                if eng == "act":
                                    R.op("act", lambda h, src=src, dst=dst: h.activation(out=dst, in_=src, func=AF.Copy), reads=[B["pb%d" % bnk]], writes=[B["PQ"]])
                                else:
                                    R.op("dve", lambda h, src=src, dst=dst: h.tensor_copy(out=dst, in_=src), reads=[B["pb%d" % bnk]], writes=[B["PQ"]])
                        NXS = 3 if RR == 2 else 2
                        XB0 = (2, 4, 0)
                        xbufs_of = lambda xs: [B["pb0"], B["pb1"]] if xs == 2 else [B["X%d" % xs]]

                        def s1(ub):
                            sl = ub % NXS
                            b0 = XB0[sl]
                            xbufs = xbufs_of(sl)
                            for ui in range(4):
                                u = ub * 4 + ui
                                dstp = ps[:, b0 + ui // 2, (ui % 2) * 256:(ui % 2) * 256 + 256]
                                R.op("pe", lambda h, u=u, dstp=dstp: h.matmul(dstp, lhsT=PQ[:, 0, u * RR:(u + 1) * RR, :].rearrange("p d s -> p (d s)"), rhs=F1ab[:, 0:256],
                                                                              start=True, stop=False),
                                     reads=[B["PQ"]], writes=xbufs)
                                R.op("pe", lambda h, u=u, dstp=dstp: h.matmul(dstp, lhsT=PQ[:, 1, u * RR:(u + 1) * RR, :].rearrange("p d s -> p (d s)"), rhs=F1ab[:, 256:512],
                                                                              start=False, stop=True),
                                     reads=[B["PQ"]], writes=xbufs)
                        def s2(ub):
                            xs = ub % NXS
                            b0 = XB0[xs]
                            sl = ub % 2
                            X = ps[:, b0:b0 + 2, :].rearrange("p b (u a k) -> p (b u) a k", u=2, a=2)
                            R.op("act", lambda h, sl=sl, X=X: h.activation(out=X2[:, sl, :, :, :], in_=X, func=AF.Copy), reads=xbufs_of(xs), writes=[B["X2%d" % sl]])
                            for mi in range(2):
                                R.op("dve", lambda h, sl=sl, mi=mi: h.tensor_tensor(out=mt[:, sl, mi, :, :, :], in0=X2[:, sl, :, :, :],
                                                                                   in1=twb[ji][:, mi, :, :].unsqueeze(1).to_broadcast([128, 4, 2, 128]), op=ALU.mult),
                                     reads=[B["X2%d" % sl]], writes=[B["mt%d%d" % (sl, mi)]])
                            for bi in range(2):
                                R.op("dve", lambda h, sl=sl, bi=bi: h.tensor_tensor(out=Ab[:, sl, :, bi, :], in0=mt[:, sl, bi, :, 0, :], in1=mt[:, sl, bi, :, 1, :], op=ALU.add),
                                     reads=[B["mt%d%d" % (sl, bi)]], writes=[B["Ab%d" % sl]])
                            YB = [6, 7, 0, 1]
                            UPB = 512 // NK2
                            for ui in range(4):
                                u = ub * 4 + ui
                                ycol = (u % UPB) * NK2
                                for r in range(RR):
                                    yb = YB[r]
                                    for bi in range(2):
                                        R.op("pe", lambda h, sl=sl, ui=ui, r=r, bi=bi, yb=yb, ycol=ycol: h.matmul(
                                            ps[:, yb, ycol:ycol + NK2], lhsT=Ab[r * N2:(r + 1) * N2, sl, ui, bi, :], rhs=F2[ji][r * N2:(r + 1) * N2, bi, 0:NK2],
                                            start=(bi == 0), stop=(bi == 1), tile_position=(r * N2, 0), skip_group_check=True),
                                             reads=[B["Ab%d" % sl]], writes=[B["pb%d" % yb]])
                                if u % UPB == UPB - 1:
                                    u0 = u - (UPB - 1)
                                    for r in range(RR):
                                        yb = YB[r]
                                        csl = slice(u0 * RR + r, (u0 + UPB) * RR, RR)
                                        R.op("dve", lambda h, yb=yb, csl=csl, gp2=gp2: h.tensor_tensor(out=fng[:, :, csl],
                                                                                            in0=ps[:, yb, :].rearrange("p (c k) -> p k c", k=NK2),
                                                                                            in1=GFg[:, gp2, 0:NK2, csl], op=ALU.mult),
                                             reads=[B["pb%d" % yb], B["GFg%d" % gp2]], writes=[B["fng"]])
                        LA = NXS - 1
                        for ub in range(min(LA, NU // 4)):
                            s1(ub)
                        for ub in range(NU // 4):
                            if ub + LA < NU // 4:
                                s1(ub + LA)
                            s2(ub)
                        for t4 in range(NK2 // 4):
                            tb = (0, 1)[t4 % 2]
                            for i in range(4):
                                R.op("pe", lambda h, t4=t4, i=i, tb=tb: h.transpose(out=psb(tb)[:, i * 128:(i + 1) * 128], in_=fng[:, t4 * 4 + i, :], identity=ident[:, :]),
                                     reads=[B["fng"]], writes=[B["pb%d" % tb]])
                            if t4 % 2 == 0:
                                R.op("act", lambda h, g=g, t4=t4, tb=tb: h.activation(out=mixT[:, 4 + g, t4 * 512:(t4 + 1) * 512], in_=psb(tb)[:, 0:512], func=AF.Copy),
                                     reads=[B["pb%d" % tb]], writes=[B["mixT"]])
                            else:
                                R.op("dve", lambda h, g=g, t4=t4, tb=tb: h.tensor_copy(out=mixT[:, 4 + g, t4 * 512:(t4 + 1) * 512], in_=psb(tb)[:, 0:512]),
                                     reads=[B["pb%d" % tb]], writes=[B["mixT"]])
                    R.emit()
                    B.clear()
                if dbg == "C":
                    dma(dmix.ap()[:, 0:8 * NQ], mixT[:, :, :].rearrange("p a b -> p (a b)"), "g4")
                    R.emit()
                    return nc

                with contextlib.ExitStack() as P:
                    wo_b = sbt(P, "wo_b", [128, 8, D], BF16)
                    wg_b = sbt(P, "wg_b", [128, 8, D], BF16)
                    wp_b = sbt(P, "wp_b", [128, 2, D], BF16)
                    load_weight(None, wo_b, wout.ap(), 8, D, "wo")
                    load_weight(None, wg_b, wgate.ap(), 8, D, "wg")
                    load_weight(None, wp_b, wproj.ap(), 2, D, "wp")
                    xt = sbt(P, "xtD", [128, 2, D], F32)
                    ptl = sbt(P, "ptl", [128, 2, 256], F32)
                    pbf = sbt(P, "pbf", [128, 2, 256], BF16)
                    pT = sbt(P, "pT", [128, 2, 2, 128], BF16)
                    h1 = sbt(P, "h1", [128, 2, D], F32)
                    junk = sbt(P, "junkD", [128, D], BF16)
                    ssD = sbt(P, "ssD", [128, 2, 6], F32)
                    hnb = sbt(P, "hnb", [128, 2, D], BF16)
                    hnT = sbt(P, "hnT", [128, 2, 8, 128], BF16)
                    sg = sbt(P, "sg", [128, 2, D], F32)
                    gp = sbt(P, "gp", [128, 2, D], F32)
                    NT = NQ // 128

                    def stA1(t):
                        s = t % 2
                        dma(xt[:, s, :], x_ap[t * 128:(t + 1) * 128, :], "dx%d" % s, writes=[B["xt%d" % s]])
                        dma(ptl[:, s, :], pj[ji].ap()[t * 128:(t + 1) * 128, :], "dp%d" % s, writes=[B["ptl%d" % s]])
                        R.op("dve", lambda h, s=s: h.memset(ssD[:, s, 0:1], 0.0), writes=[B["ssa%d" % s]])
                        R.op("dve", lambda h, s=s: h.memset(ssD[:, s, 3:4], 0.0), writes=[B["ssb%d" % s]])
                        for half in range(2):
                            for c in range(8):
                                R.op("pe", lambda h, t=t, c=c, half=half: h.matmul(ps[:, half, :], lhsT=mixT[:, c, t * 128:(t + 1) * 128], rhs=wo_b[:, c, half * 512:(half + 1) * 512],
                                                                                   start=(c == 0), stop=(c == 7)),
                                     reads=[B["wo_%d" % c]], writes=[B["pb01"]])
                        R.op("dve", lambda h, s=s: h.tensor_tensor(out=h1[:, s, :], in0=ps[:, 0:2, :].rearrange("p a b -> p (a b)"), in1=xt[:, s, :], op=ALU.add),
                             reads=[B["pb01"], B["xt%d" % s]], writes=[B["h1%d" % s]])
                        R.op("act", lambda h, s=s: h.activation(out=junk[:, :], in_=h1[:, s, :], func=AF.Square, accum_out=ssD[:, s, 0:1]),
                             reads=[B["h1%d" % s]], writes=[B["junk"], B["ssa%d" % s]])
                        rstd_ops(ssD[:, s, 0:1], ssD[:, s, 2:3], 1.0 / D, epsc[:, 0:1], "rD%d" % s, [B["ssa%d" % s]], B["rsa%d" % s], ssD[:, s, 1:2])
                        R.op("dve", lambda h, s=s: h.scalar_tensor_tensor(out=hnb[:, s, :], in0=h1[:, s, :], scalar=ssD[:, s, 2:3], in1=gvec[:, 1, :], op0=ALU.mult, op1=ALU.mult),
                             reads=[B["h1%d" % s], B["rsa%d" % s]], writes=[B["hnb%d" % s]])
                        R.op("dve", lambda h, s=s: h.tensor_copy(out=pbf[:, s, :], in_=ptl[:, s, :]), reads=[B["ptl%d" % s]], writes=[B["pbf%d" % s]])

                    def stA2(t):
                        s = t % 2
                        for c in range(8):
                            R.op("pe", lambda h, s=s, c=c: h.transpose(out=psb(2)[:, c * 128:(c + 1) * 128], in_=hnb[:, s, c * 128:(c + 1) * 128], identity=ident[:, :]),
                                 reads=[B["hnb%d" % s]], writes=[B["pb2"]])
                        R.op("act", lambda h, s=s: h.activation(out=hnT[:, s, :, :], in_=psb(2).rearrange("p (c t) -> p c t", c=8), func=AF.Copy),
                             reads=[B["pb2"]], writes=[B["hnT%d" % s]])
                        for c in range(2):
                            R.op("pe", lambda h, s=s, c=c: h.transpose(out=psb(5)[:, c * 128:(c + 1) * 128], in_=pbf[:, s, c * 128:(c + 1) * 128], identity=ident[:, :]),
                                 reads=[B["pbf%d" % s]], writes=[B["pb5"]])
                        R.op("dve", lambda h, s=s: h.tensor_copy(out=pT[:, s, :, :], in_=psb(5)[:, 0:256].rearrange("p (c t) -> p c t", c=2)),
                             reads=[B["pb5"]], writes=[B["pT%d" % s]])

                    def stB(t):
                        s = t % 2
                        for half in range(2):
                            for c in range(8):
                                R.op("pe", lambda h, s=s, c=c, half=half: h.matmul(ps[:, 3 + half, :], lhsT=hnT[:, s, c, :], rhs=wg_b[:, c, half * 512:(half + 1) * 512],
                                                                                   start=(c == 0), stop=(c == 7)),
                                     reads=[B["hnT%d" % s], B["wg_%d" % c]], writes=[B["pb34"]])
                        R.op("act", lambda h, s=s: h.activation(out=sg[:, s, :], in_=ps[:, 3:5, :].rearrange("p a b -> p (a b)"), func=AF.Sigmoid),
                             reads=[B["pb34"]], writes=[B["sg%d" % s]])
                        for half in range(2):
                            for c in range(2):
                                R.op("pe", lambda h, s=s, c=c, half=half: h.matmul(ps[:, 6 + half, :], lhsT=pT[:, s, c, :], rhs=wp_b[:, c, half * 512:(half + 1) * 512],
                                                                                   start=(c == 0), stop=(c == 1)),
                                     reads=[B["pT%d" % s], B["wp_%d" % c]], writes=[B["pb67"]])
                        R.op("dve", lambda h, s=s: h.tensor_tensor(out=gp[:, s, :], in0=ps[:, 6:8, :].rearrange("p a b -> p (a b)"), in1=sg[:, s, :], op=ALU.mult),
                             reads=[B["pb67"], B["sg%d" % s]], writes=[B["gp%d" % s]])
                        R.op("dve", lambda h, s=s: h.tensor_tensor(out=gp[:, s, :], in0=gp[:, s, :], in1=h1[:, s, :], op=ALU.add),
                             reads=[B["gp%d" % s], B["h1%d" % s]], writes=[B["gp%d" % s]])
                        R.op("act", lambda h, s=s: h.activation(out=junk[:, :], in_=gp[:, s, :], func=AF.Square, accum_out=ssD[:, s, 3:4]),
                             reads=[B["gp%d" % s]], writes=[B["junk"], B["ssb%d" % s]])
                        rstd_ops(ssD[:, s, 3:4], ssD[:, s, 5:6], 1.0 / D, epsc[:, 0:1], "rE%d" % s, [B["ssb%d" % s]], B["rsb%d" % s], ssD[:, s, 4:5])
                        R.op("dve", lambda h, s=s: h.scalar_tensor_tensor(out=sg[:, s, :], in0=gp[:, s, :], scalar=ssD[:, s, 5:6], in1=gvec[:, 2, :], op0=ALU.mult, op1=ALU.mult),
                             reads=[B["gp%d" % s], B["rsb%d" % s]], writes=[B["sg%d" % s]])
                        dma(yj[ji].ap()[t * 128:(t + 1) * 128, :], sg[:, s, :], "y%d" % s, eng="pool", reads=[B["sg%d" % s]])

                    stA1(0)
                    stA2(0)
                    for t in range(NT):
                        if t + 1 < NT:
                            stA1(t + 1)
                        stB(t)
                        if t + 1 < NT:
                            stA2(t + 1)
                    R.emit()
                    B.clear()
    return nc


def _rel_bucket_table():
    rel = np.arange(-639, 640, dtype=np.int32)
    ret = np.where(rel > 0, 16, 0)
    n = np.abs(rel)
    nf = np.maximum(n, 1).astype(np.float32)
    large = 8 + (np.log(nf / np.float32(8)) / np.float32(math.log(128 / 8)) * np.float32(8)).astype(np.int32)
    large = np.minimum(large, 15)
    return ret + np.where(n < 8, n, large)


def _host_consts():
    bf = ml_dtypes.bfloat16
    c = {}
    c["ident"] = np.eye(128, dtype=np.float32).astype(bf)
    c["Jm"] = np.ascontiguousarray(np.eye(128, dtype=np.float32)[::-1])
    i = np.arange(128)
    ang = 2 * np.pi * np.outer(i, i) / 128.0
    Cm, Sm = np.cos(ang), np.sin(ang)
    c["CS"] = (np.concatenate([Cm, Sm], 1) / np.sqrt(128.0)).astype(np.float32)
    c["F1ab"] = np.concatenate([Cm, Sm, -Sm, Cm], 1).astype(np.float32).astype(bf)
    bucket = _rel_bucket_table()
    per_core = []
    for core in range(NCORES):
        r = core % 4
        d = {}
        for ji, job in enumerate(JOBS):
            N = job["NKV"]; N2 = job["N2"]; RR = job["R"]; NK2 = job["NK2"]
            off = 0 if ji == 0 else 2048 * r
            s2 = np.arange(N2)[:, None].astype(np.float64)
            k1 = np.arange(128)[None, :].astype(np.float64)
            th = 2 * np.pi * (((s2 + off) * (k1 + off)) % N) / N
            twr, twi = np.cos(th), -np.sin(th)
            T = np.stack([twr, twi, -twr], 1)
            d["tw%d" % ji] = np.ascontiguousarray(np.tile(T, (RR, 1, 1))).astype(np.float32)
            k2 = np.arange(NK2)[None, :].astype(np.float64)
            ph = 2 * np.pi * ((s2 * k2) % N2) / N2
            F = np.zeros((N2, 2, 32), np.float64)
            F[:, 0, :NK2] = np.cos(ph) / np.sqrt(N)
            F[:, 1, :NK2] = np.sin(ph) / np.sqrt(N)
            d["F2%d" % ji] = np.ascontiguousarray(np.tile(F, (RR, 1, 1))).astype(np.float32).astype(bf)
        OHG = np.zeros((32, 3, 1280), np.float32)
        tp = np.arange(1279)
        rel = 639 - tp
        OHG[bucket[rel + 639], 0, tp] = 8.0
        if r < 3:
            OHG[:, 1, :] = OHG[:, 0, :]
        else:
            OHG[15, 1, :1279] = 8.0
        if r > 0:
            OHG[:, 2, :] = OHG[:, 0, :]
        else:
            OHG[31, 2, :1279] = 8.0
        d["OHG"] = OHG
        OHC = np.zeros((32, NCOL), np.float32)
        for j in range(16, 64):
            OHC[31 if j < 64 - 16 * r else 15, j] = 1.0
        OHC[15, 64] = 1.0
        OHC[31, 65] = 1.0
        d["OHC"] = OHC
        per_core.append(d)
    return c, per_core


_CACHE = {}


def kernel(x_prompt, x_sample, p_prompt, p_sample, norm_mix_g, w_in, lambda_params, subln_g, rel_bias,
           w_fourier, w_out, ple_norm_g, w_ple_gate, w_ple_proj, final_norm_g):
    f = lambda a: np.ascontiguousarray(np.asarray(a, dtype=np.float32))
    x_prompt, x_sample, p_prompt, p_sample = f(x_prompt), f(x_sample), f(p_prompt), f(p_sample)
    if "nc" not in _CACHE:
        _CACHE["nc"] = build_program()
        _CACHE["consts"] = _host_consts()
    nc = _CACHE["nc"]
    cglob, cper = _CACHE["consts"]
    shared = dict(win=f(w_in)[0], wout=f(w_out)[0], wgate=f(w_ple_gate)[0], wproj=f(w_ple_proj)[0], wf=f(w_fourier)[0],
                  gmix=f(norm_mix_g).reshape(1, D), gple=f(ple_norm_g).reshape(1, D), gfin=f(final_norm_g).reshape(1, D),
                  gsub=f(subln_g).reshape(1, 128), lamp=f(lambda_params).reshape(1, 256), relb=f(rel_bias))
    shared.update(cglob)
    in_maps = []
    for core in range(NCORES):
        b, r = core // 4, core % 4
        m = dict(shared)
        m.update(cper[core])
        m["xp"] = x_prompt[core]
        m["pp"] = p_prompt[0, core]
        m["xs"] = np.ascontiguousarray(np.roll(x_sample[b], -2048 * r, axis=0))
        m["psm"] = np.ascontiguousarray(p_sample[0, b, 2048 * r:2048 * (r + 1)])
        in_maps.append(m)
    if _CACHE.get("prep_only"):
        return in_maps
    res = run_bass_kernel_spmd(nc, in_maps, core_ids=list(range(NCORES)))
    yp = np.stack([res.results[c]["yp"] for c in range(NCORES)], 0)
    ys = np.zeros((2, 8192, D), np.float32)
    for core in range(NCORES):
        b, r = core // 4, core % 4
        ys[b, 2048 * r:2048 * (r + 1)] = res.results[core]["ys"]
    return (yp.astype(np.float32), ys)
```
